# Optimizing a Trainium2 kernel written in Bass

```python
import jax, jax.numpy as jnp
from jax import lax
import numpy as np

D_MODEL = 1024
BATCH = 2
SEQ = 8192
DEPTH = 2

CHUNK = 64
MIX_WIDTH = D_MODEL
CONV_WIDTH = MIX_WIDTH // 2
RET_WIDTH = MIX_WIDTH - CONV_WIDTH
RET_HEADS = 4
RET_HEAD_DIM = RET_WIDTH // RET_HEADS
CONV_KERNEL = 31
D_FF = 4 * D_MODEL
ROPE_BASE = 10000.0
EPS = 1e-6
IN_WIDTH = 2 * CONV_WIDTH + 4 * RET_WIDTH

kernel_name = "hybrid_conv_retention_encoder"


def rms_norm(x, g):
    xf = x.astype(jnp.float32)
    y = xf * lax.rsqrt(jnp.mean(xf * xf, axis=-1, keepdims=True) + EPS)
    return (y * g.astype(jnp.float32)).astype(x.dtype)


def layer_norm(x, g, b):
    xf = x.astype(jnp.float32)
    mu = jnp.mean(xf, axis=-1, keepdims=True)
    var = jnp.mean(jnp.square(xf - mu), axis=-1, keepdims=True)
    y = (xf - mu) * lax.rsqrt(var + EPS)
    return (y * g.astype(jnp.float32) + b.astype(jnp.float32)).astype(x.dtype)


def conv_group(u, w_dw, b_dw, ln_g, ln_b):
    a, gate = jnp.split(u, 2, axis=-1)
    h = a * jax.nn.sigmoid(gate)
    h = lax.conv_general_dilated(
        h, w_dw[:, None, :], window_strides=(1,),
        padding=[(CONV_KERNEL - 1, 0)],
        dimension_numbers=("NWC", "WIO", "NWC"),
        feature_group_count=CONV_WIDTH) + b_dw
    h = layer_norm(h, ln_g, ln_b)
    return jax.nn.silu(h)


def rotary(x, cos, sin):
    x1, x2 = jnp.split(x, 2, axis=-1)
    c = cos[None, :, None, :]
    s = sin[None, :, None, :]
    return jnp.concatenate([x1 * c - x2 * s, x1 * s + x2 * c], axis=-1)


def chunk_retention(q, k, v):
    b, s, h, d = q.shape
    n = s // CHUNK
    dt = q.dtype
    q = q.reshape(b, n, CHUNK, h, d)
    k = k.reshape(b, n, CHUNK, h, d)
    v = v.reshape(b, n, CHUNK, h, d)
    log_gamma = jnp.log(1.0 - jnp.exp2(-5.0 - jnp.arange(h, dtype=jnp.float32)))
    idx = jnp.arange(CHUNK, dtype=jnp.float32)
    dist = jnp.abs(idx[:, None] - idx[None, :])
    d_intra = jnp.exp(log_gamma[:, None, None] * dist[None]).astype(dt)
    w_key = jnp.exp(log_gamma[:, None] * (CHUNK - 1 - idx)[None]).astype(dt)
    w_qry = jnp.exp(log_gamma[:, None] * (idx + 1.0)[None]).astype(dt)
    g_chunk = jnp.exp(log_gamma * CHUNK).astype(dt)[None, :, None, None]

    scores = jnp.einsum("bnihd,bnjhd->bnhij", q, k) * d_intra
    o_intra = jnp.einsum("bnhij,bnjhe->bnihe", scores, v)

    kv = jnp.einsum("bnjhd,hj,bnjhe->nbhde", k, w_key, v)

    def step(state, kv_c):
        return state * g_chunk + kv_c, state

    _, s_prev = lax.scan(step, jnp.zeros((b, h, d, d), dt), kv)
    o_inter = jnp.einsum("bnihd,hi,nbhde->bnihe", q, w_qry, s_prev)
    return (o_intra + o_inter).reshape(b, s, h, d)


def retention_group(q, k, v, g, norm_g, cos, sin):
    b, s, _ = q.shape
    q = rotary(q.reshape(b, s, RET_HEADS, RET_HEAD_DIM), cos, sin)
    k = rotary(k.reshape(b, s, RET_HEADS, RET_HEAD_DIM), cos, sin) * (RET_HEAD_DIM ** -0.5)
    v = v.reshape(b, s, RET_HEADS, RET_HEAD_DIM)
    o = chunk_retention(q, k, v)
    o = rms_norm(o, norm_g.reshape(RET_HEADS, RET_HEAD_DIM))
    return o.reshape(b, s, RET_WIDTH) * jax.nn.silu(g)


def setup_inputs(seed: int = 0) -> dict:
    key = jax.random.key(seed)
    ks = jax.random.split(key, 16)
    f32 = jnp.float32
    nrm = lambda k, shape, scale: jax.random.normal(k, shape, f32) * scale
    return {
        "x": nrm(ks[0], (BATCH, SEQ, D_MODEL), 1.0),
        "norm1_g": 1.0 + nrm(ks[1], (DEPTH, D_MODEL), 0.02),
        "w_in": nrm(ks[2], (DEPTH, D_MODEL, IN_WIDTH), D_MODEL ** -0.5),
        "conv_w": nrm(ks[3], (DEPTH, CONV_KERNEL, CONV_WIDTH), CONV_KERNEL ** -0.5),
        "conv_b": nrm(ks[4], (DEPTH, CONV_WIDTH), 0.02),
        "conv_ln_g": 1.0 + nrm(ks[5], (DEPTH, CONV_WIDTH), 0.02),
        "conv_ln_b": nrm(ks[6], (DEPTH, CONV_WIDTH), 0.02),
        "ret_norm_g": 1.0 + nrm(ks[7], (DEPTH, RET_WIDTH), 0.02),
        "w_out": nrm(ks[8], (DEPTH, MIX_WIDTH, D_MODEL), MIX_WIDTH ** -0.5),
        "norm2_g": 1.0 + nrm(ks[9], (DEPTH, D_MODEL), 0.02),
        "w_ff1": nrm(ks[10], (DEPTH, D_MODEL, D_FF), D_MODEL ** -0.5),
        "w_ff2": nrm(ks[11], (DEPTH, D_FF, D_MODEL), D_FF ** -0.5),
        "final_g": 1.0 + nrm(ks[12], (D_MODEL,), 0.02),
    }


def reference(x, norm1_g, w_in, conv_w, conv_b, conv_ln_g, conv_ln_b, ret_norm_g,
              w_out, norm2_g, w_ff1, w_ff2, final_g):
    s = x.shape[1]
    pos = jnp.arange(s, dtype=jnp.float32)
    inv_freq = ROPE_BASE ** (-jnp.arange(0, RET_HEAD_DIM, 2, dtype=jnp.float32) / RET_HEAD_DIM)
    ang = pos[:, None] * inv_freq[None, :]
    cos = jnp.cos(ang).astype(x.dtype)
    sin = jnp.sin(ang).astype(x.dtype)
    split_at = [2 * CONV_WIDTH + i * RET_WIDTH for i in range(4)]

    for l in range(DEPTH):
        h = rms_norm(x, norm1_g[l])
        u = h @ w_in[l]
        u_conv, q, k, v, g = jnp.split(u, split_at, axis=-1)
        a_out = conv_group(u_conv, conv_w[l], conv_b[l], conv_ln_g[l], conv_ln_b[l])
        b_out = retention_group(q, k, v, g, ret_norm_g[l], cos, sin)
        x = x + jnp.concatenate([a_out, b_out], axis=-1) @ w_out[l]
        h = rms_norm(x, norm2_g[l])
        x = x + jnp.square(jax.nn.relu(h @ w_ff1[l])) @ w_ff2[l]
    return rms_norm(x, final_g)
```

```python
import numpy as np
import ml_dtypes
import concourse.bass as bass
import concourse.mybir as mybir
from concourse.bass_utils import run_bass_kernel_spmd

F32 = mybir.dt.float32
BF16 = mybir.dt.bfloat16
ALU = mybir.AluOpType
AF = mybir.ActivationFunctionType

NL = 2
D = 1024
SEQ = 8192
TOK = 2048
T = 512
NT = TOK // T
NB = TOK // 128
EPS = 1e-6
CK = 31
PL = 156
NPAR = 2 * PL + 8
NSLOT = 4
DBG = 99
CM_MT = 0
CM_DT = 512
CM_W2 = 1024
CM_WT = 1028
CM_ID = 1092
CM_COEF = 1220
CM_SEL = 1252
CM_N = 1260
XW = 640


class Tl:
    all = []

    def __init__(self, psum=False):
        Tl.all.append(self)
        self.psum = psum
        self.w = None
        self.r = []
        self.sem = None
        self.cnt = 0


class B:
    def __init__(self, nc):
        self.nc = nc
        self.eng = {}
        for name, h in (("pe", nc.tensor), ("dve", nc.vector), ("act", nc.scalar),
                        ("pool", nc.gpsimd), ("sp", nc.sync)):
            sem = nc.alloc_semaphore("s_" + name)
            self.eng[name] = dict(h=h, sem=sem, cnt=0, seen={})
        self.nsem = 0

    def _waits(self, e, deps):
        E = self.eng[e]
        best = {}
        for d in deps:
            if d is None:
                continue
            sem, val = d
            if e == "pe" and sem is self.eng["pe"]["sem"]:
                continue
            k = id(sem)
            if k not in best or best[k][1] < val:
                best[k] = (sem, val)
        for k, (sem, val) in best.items():
            if E["seen"].get(k, 0) >= val:
                continue
            E["h"].wait_ge(sem, val)
            E["seen"][k] = val

    def _deps(self, R, W, deps, e=None):
        d = list(deps)
        for t in R:
            d.append(t.w)
            if t.psum:
                own = self.eng[e]["sem"]
                d.extend(tok for tok in t.r if tok[0] is not own)
        for t in W:
            d.append(t.w)
            d.extend(t.r)
        return d

    def _commit(self, tok, R, W):
        for t in R:
            t.r.append(tok)
        for t in W:
            t.w = tok
            t.r = []

    def op(self, e, fn, R=(), W=(), deps=()):
        E = self.eng[e]
        self._waits(e, self._deps(R, W, deps, e))
        ins = fn(E["h"])
        ins.then_inc(E["sem"], 1)
        E["cnt"] += 1
        tok = (E["sem"], E["cnt"])
        self._commit(tok, R, W)
        return tok

    def mm(self, Wt, out, pairs, R=(), start=True, stop=True, skip=False, deps=()):
        E = self.eng["pe"]
        self._waits("pe", self._deps(R, [Wt], deps, "pe"))
        n = len(pairs)
        ins = None
        for i, (l, r) in enumerate(pairs):
            kw = {}
            if skip:
                kw["skip_group_check"] = True
            ins = self.nc.tensor.matmul(out, l, r, start=(start and i == 0),
                                        stop=(stop and i == n - 1), **kw)
        ins.then_inc(E["sem"], 1)
        E["cnt"] += 1
        tok = (E["sem"], E["cnt"])
        self._commit(tok, R, [Wt])
        return tok

    def tr(self, Wt, out, in_, ident, R=()):
        E = self.eng["pe"]
        self._waits("pe", self._deps(R, [Wt], (), "pe"))
        ins = self.nc.tensor.transpose(out, in_, ident)
        ins.then_inc(E["sem"], 1)
        E["cnt"] += 1
        tok = (E["sem"], E["cnt"])
        self._commit(tok, R, [Wt])
        return tok

    def dma_multi(self, e, lst, t):
        E = self.eng[e]
        self._waits(e, self._deps((), [t], (), e))
        if t.sem is None:
            self.nsem += 1
            t.sem = self.nc.alloc_semaphore(f"d{self.nsem}")
        for out, in_ in lst:
            E["h"].dma_start(out=out, in_=in_).then_inc(t.sem, 16)
            t.cnt += 16
        tok = (t.sem, t.cnt)
        self._commit(tok, (), [t])
        return tok

    def dma(self, e, out, in_, R=(), W=(), deps=()):
        E = self.eng[e]
        self._waits(e, self._deps(R, W, deps, e))
        t = (list(W) + list(R))[0]
        if t.sem is None:
            self.nsem += 1
            t.sem = self.nc.alloc_semaphore(f"d{self.nsem}")
        E["h"].dma_start(out=out, in_=in_).then_inc(t.sem, 16)
        t.cnt += 16
        tok = (t.sem, t.cnt)
        self._commit(tok, R, W)
        return tok


def _bc(ap, shape, axes):
    for a in axes:
        ap = ap.unsqueeze(a)
    return ap.broadcast_to(list(shape))


def build(phases, store_x=False, final=False):
    nc = bass.Bass("TRN2", target_bir_lowering=False)
    Tl.all = []
    b = B(nc)
    lay = sorted({l for _, l in phases})
    has_p1 = [l for p, l in phases if p == "P1"]
    has_p2 = [l for p, l in phases if p == "P2"]

    def din(name, shape, dt=F32):
        return nc.dram_tensor(name, list(shape), dt, kind="ExternalInput").ap()

    def dout(name, shape, dt=F32):
        return nc.dram_tensor(name, list(shape), dt, kind="ExternalOutput").ap()

    w_in = {l: din(f"w_in{l}", [128, 8, 3072]) for l in lay}
    w_out = {l: din(f"w_out{l}", [128, 8, 1024]) for l in has_p2}
    w_ff1 = {l: din(f"w_ff1{l}", [128, 8, 4096]) for l in has_p2}
    w_ff2 = {l: din(f"w_ff2{l}", [128, 32, 1024]) for l in has_p2}
    params_d = din("params", [128, NPAR])
    ctab_d = din("ctab", [128, NB, 64])
    stab_d = din("stab", [128, NB, 64])
    cmisc_d = din("cmisc", [128, CM_N])
    xT_d = din("xT", [128, 8, TOK])
    if has_p2:
        xch_in_d = din("xch_in", [128, 8, XW])
        kv_in_d = din("kv_in", [128, NB, 2, 512], BF16)
    if has_p1:
        xch_out_d = dout("xch_out", [128, XW])
        kv_out_d = dout("kv_out", [128, NB, 2, 512], BF16)
    if store_x or final:
        xT_out_d = dout("xT_out", [128, 8, TOK])

    def sb(name, shape, dt=F32):
        return nc.alloc_sbuf_tensor(name, list(shape), dt)

    def ps(name, shape, dt=F32):
        return nc.alloc_psum_tensor(name, list(shape), dt)

    params = sb("params_sb", [128, NPAR]); params_t = Tl()
    ctab = sb("ctab_sb", [128, NB, 64]); ctab_t = Tl()
    stab = sb("stab_sb", [128, NB, 64]); stab_t = Tl()
    cmisc = sb("cmisc_sb", [128, CM_N]); cmisc_t = Tl()
    ident = sb("ident_sb", [128, 128], BF16); ident_t = Tl()
    ones_d = sb("ones_d", [128, 128], BF16)
    ones_c = sb("ones_c", [128, 128], BF16)
    ones_h = sb("ones_h", [128, 128], BF16)
    c_toks = [
        b.dma("sp", params[:, :], params_d, W=[params_t]),
        b.dma("sp", ctab[:, :, :], ctab_d, W=[ctab_t]),
        b.dma("sp", stab[:, :, :], stab_d, W=[stab_t]),
        b.dma("sp", cmisc[:, :], cmisc_d, W=[cmisc_t]),
        b.dma("pool", ident[:, :], cmisc_d[:, CM_ID:CM_ID + 128], W=[ident_t]),
    ]
    c_toks.append(b.op("dve", lambda h: h.memset(ones_d[:, :], 1.0 / 1024)))
    c_toks.append(b.op("dve", lambda h: h.memset(ones_c[:, :], 1.0 / 512)))
    c_toks.append(b.op("dve", lambda h: h.memset(ones_h[:, :], 1.0 / 128)))
    for e in ("pe", "dve", "act"):
        b._waits(e, c_toks)
    MT = cmisc[:, CM_MT:CM_MT + 512]
    DTAB = cmisc[:, CM_DT:CM_DT + 512]
    W2TAB = cmisc[:, CM_W2:CM_W2 + 4]

    def pcol(l, off):
        return params[:, l * PL + off: l * PL + off + 1]

    Xb = [sb(f"X{i}", [128, 8, T]) for i in range(2)]
    Xt_ = [Tl() for _ in range(2)]
    xn = sb("xn", [128, 8, T], BF16); xn_t = Tl()
    sq = [sb(f"sq{i}", [128, T], BF16) for i in range(2)]; sq_t = [Tl(), Tl()]
    rstd = sb("rstd", [128, T]); rstd_t = Tl()
    rt1 = sb("rt1", [128, 512]); rt1_t = Tl()
    rt2 = sb("rt2", [128, 512]); rt2_t = Tl()
    slots = [sb(f"wslot{i}", [128, 4096], BF16) for i in range(NSLOT)]
    slot_t = [Tl() for _ in range(NSLOT)]
    pa = [ps(f"pa{i}", [128, 512]) for i in range(3)]; pa_t = [Tl(True) for _ in range(3)]
    pS1 = ps("pS1", [128, 512]); pS1_t = Tl(True)
    pst = ps("pst", [128, 512]); pst_t = Tl(True)
    pS = ps("pS", [128, 512]); pS_t = Tl(True)
    po = ps("po", [128, 512]); po_t = Tl(True)
    ptr_all = ps("ptr", [128, 1024], BF16)
    ptr = [ptr_all[:, 0:512], ptr_all[:, 512:1024]]; ptr_t = [Tl(True)] * 2
    pa_rr = [0]

    def next_pa():
        i = pa_rr[0] % 3
        pa_rr[0] += 1
        return pa[i], pa_t[i]

    if has_p1:
        ktok = sb("ktok", [128, 4, 512], BF16); ktok_t = [Tl() for _ in range(4)]
        vtok = [sb(f"vtok{i}", [128, 512], BF16) for i in range(2)]; vtok_t = [Tl(), Tl()]
        vW = [sb(f"vW{i}", [128, 512], BF16) for i in range(2)]; vW_t = [Tl(), Tl()]
        xch_sb = sb("xch_sb", [128, XW]); xch_sb_t = Tl()
        sgt = sb("sgt", [128, 128]); sgt_t = Tl()
    if has_p2:
        xr = [sb(f"xr{i}", [128, XW]) for i in range(2)]; xr_t = [Tl(), Tl()]
        hg = sb("hg", [128, 4, 30 + T]); hg_t = [Tl() for _ in range(4)]; halo_t = Tl()
        sig = [sb(f"sig{i}", [128, T]) for i in range(2)]; sig_t = [Tl(), Tl()]
        acc = sb("acc", [128, 4, T]); acc_t = [Tl() for _ in range(4)]
        abf = [sb(f"abf{i}", [128, T], BF16) for i in range(2)]; abf_t = [Tl(), Tl()]
        asq = [sb(f"asq{i}", [128, T], BF16) for i in range(2)]; asq_t = [Tl(), Tl()]
        mean = sb("mean", [128, T]); mean_t = Tl()
        rstdc = sb("rstdc", [128, T]); rstdc_t = Tl()
        lnt = [sb(f"lnt{i}", [128, T]) for i in range(2)]; lnt_t = [Tl(), Tl()]
        Abuf = sb("Abuf", [128, 4, T], BF16); A_t = [Tl() for _ in range(4)]
        Bbuf = sb("Bbuf", [128, 4, T], BF16); Bb_t = Tl()
        qtok = sb("qtok", [128, 4, 512], BF16); qtok_t = [Tl() for _ in range(4)]
        sgn = sb("sgn", [128, 4, T]); sgn_t = [Tl() for _ in range(4)]
        kvt = [sb("kvt0", [128, 4, 2, 512], BF16)] * 2; kvt_t = [Tl()] * 2
        qT = sb("qT", [128, 512], BF16); qT_t = Tl()
        qdT = sb("qdT", [128, 512], BF16); qdT_t = Tl()
        kT = sb("kT", [128, 512], BF16); kT_t = Tl()
        PTm = sb("PTm", [128, 512], BF16); PTm_t = Tl()
        vw2 = sb("vw2", [128, 512], BF16); vw2_t = Tl()
        S = sb("S", [128, 512]); S_t = Tl()
        Sb_ = sb("Sb", [128, 512], BF16); Sb_t = Tl()
        osq = sb("osq", [128, 512], BF16); osq_t = Tl()
        rstdo = sb("rstdo", [128, 512]); rstdo_t = Tl()
        bt = sb("bt", [128, 512]); bt_t = Tl()
        H = sb("H", [128, 16, T], BF16); H_t = [Tl() for _ in range(16)]
        relu_s = lnt; relu_t = lnt_t
    if final:
        yout = sb("yout", [128, 8, T]); yout_t = Tl()

    steps = []

    def step(piece, fn):
        steps.append((piece, fn))

    def run_steps():
        pidx = [i for i, s in enumerate(steps) if s[0] is not None]
        issued = 0

        def issue(j):
            si = j % NSLOT
            lst = []
            for (c0, c1, src) in steps[pidx[j]][0]:
                dst = slots[si][:, c0:c1]
                if len(src.shape) == 3:
                    dst = dst.rearrange("p (a n) -> p a n", a=src.shape[1])
                lst.append((dst, src))
            b.dma_multi("pool", lst, slot_t[si])

        k = 0
        for i, (piece, fn) in enumerate(steps):
            if piece is not None:
                while issued < len(pidx) and issued <= k + NSLOT - 1:
                    issue(issued)
                    issued += 1
                si = k % NSLOT
                fn(slots[si], slot_t[si])
                k += 1
            else:
                fn(None, None)

    rq = sb("rq", [128, 512]); rq_t = Tl()

    def rsqrt(dst, dst_t, src, src_t):
        b.op("act", lambda h: h.activation(rq[:, :], src, AF.Sqrt, bias=EPS), R=[src_t], W=[rq_t])
        b.op("dve", lambda h: h.reciprocal(dst, rq[:, :]), R=[rq_t], W=[dst_t])

    def norm_tile(Xap, X_tl, goff, out3, out_tl, out_is_f32=False):
        for kc in range(8):
            i = kc % 2
            b.op("act", lambda h: h.activation(sq[i][:, :], Xap[:, kc, :], AF.Square),
                 R=[X_tl], W=[sq_t[i]])
            b.mm(pst_t, pst[:, :], [(ones_d[:, :], sq[i][:, :])], R=[sq_t[i]],
                 start=(kc == 0), stop=(kc == 7))
        rsqrt(rstd[:, :], rstd_t, pst[:, :], pst_t)
        for kc in range(8):
            b.op("dve", lambda h: h.scalar_tensor_tensor(
                out3[:, kc, :], Xap[:, kc, :], params[:, goff + kc: goff + kc + 1], rstd[:, :],
                ALU.mult, ALU.mult), R=[X_tl, rstd_t], W=[out_tl])

    def rotary(psrc, psrc_t, blk_g, dst, dst_t):
        x4 = psrc.rearrange("p (h two d) -> p h two d", h=4, two=2)
        c = ctab[:, blk_g, :]
        s = stab[:, blk_g, :]
        t1v = rt1[:, :].rearrange("p (h two d) -> p h two d", h=4, two=2)
        t2v = rt2[:, :].rearrange("p (h two d) -> p h two d", h=4, two=2)
        b.op("dve", lambda h: h.tensor_tensor(t1v, x4, _bc(c, [128, 4, 2, 64], (1, 1)), ALU.mult),
             R=[psrc_t], W=[rt1_t])
        b.op("dve", lambda h: h.scalar_tensor_tensor(
            t2v[:, :, 0, :], x4[:, :, 1, :], -1.0, _bc(s, [128, 4, 64], (1,)), ALU.mult, ALU.mult),
            R=[psrc_t], W=[rt2_t])
        b.op("dve", lambda h: h.tensor_tensor(
            t2v[:, :, 1, :], x4[:, :, 0, :], _bc(s, [128, 4, 64], (1,)), ALU.mult),
            R=[psrc_t], W=[rt2_t])
        b.op("dve", lambda h: h.tensor_tensor(dst, rt1[:, :], rt2[:, :], ALU.add),
             R=[rt1_t, rt2_t], W=[dst_t])

    def win_piece(l, c0, n=512):
        return [(0, 8 * n, w_in[l][:, :, c0:c0 + n])]

    def emit_P1(l, t, Xap, X_tl):
        g1 = l * PL + 0
        if DBG < 2:
            return
        step(None, lambda s, st: norm_tile(Xap, X_tl, g1, xn, xn_t))
        if DBG < 3:
            return

        def do_k(s, st):
            Wv = s[:, :].rearrange("p (a n) -> p a n", a=8)
            for blk in range(4):
                p, p_t = next_pa()
                b.mm(p_t, p[:, :], [(xn[:, kc, blk * 128:(blk + 1) * 128], Wv[:, kc, :]) for kc in range(8)],
                     R=[xn_t, st])
                rotary(p[:, :], p_t, t * 4 + blk, ktok[:, blk, :], ktok_t[blk])
                b.dma("sp", kv_out_d[:, t * 4 + blk, 0, :], ktok[:, blk, :], R=[ktok_t[blk]])
        step(win_piece(l, 1536), do_k)
        if DBG < 4:
            return

        def do_v(s, st):
            Wv = s[:, :].rearrange("p (a n) -> p a n", a=8)
            for blk in range(4):
                gb = t * 4 + blk
                i = blk % 2
                p, p_t = next_pa()
                b.mm(p_t, p[:, :], [(xn[:, kc, blk * 128:(blk + 1) * 128], Wv[:, kc, :]) for kc in range(8)],
                     R=[xn_t, st])
                b.op("act", lambda h: h.copy(vtok[i][:, :], p[:, :]), R=[p_t], W=[vtok_t[i]])
                b.dma("sp", kv_out_d[:, gb, 1, :], vtok[i][:, :], R=[vtok_t[i]])
                wt = cmisc[:, CM_WT + gb * 4: CM_WT + gb * 4 + 4]
                b.op("dve", lambda h: h.tensor_tensor(
                    vW[i][:, :].rearrange("p (h e) -> p h e", h=4), p[:, :].rearrange("p (h e) -> p h e", h=4),
                    _bc(wt, [128, 4, 128], (2,)), ALU.mult), R=[p_t], W=[vW_t[i]])
                for hh in range(4):
                    b.mm(pS1_t, pS1[:, hh * 128:(hh + 1) * 128],
                         [(ktok[:, blk, hh * 128:(hh + 1) * 128], vW[i][:, hh * 128:(hh + 1) * 128])],
                         R=[ktok_t[blk], vW_t[i]], start=(gb == 0 and hh == 0), stop=(gb == NB - 1 and hh == 3),
                         skip=True)
        step(win_piece(l, 2048), do_v)
        if DBG < 5:
            return

        if t == NT - 1:
            for hp in range(2 if DBG >= 6 else 0):
                def do_tail_piece(s, st, hp=hp):
                    Wa = s[:, 0:2048].rearrange("p (a n) -> p a n", a=8)
                    Wg = s[:, 2048:4096].rearrange("p (a n) -> p a n", a=8)
                    for j in range(2):
                        cc = hp * 2 + j
                        b.mm(pa_t[0], pa[0][:, cc * 32:(cc + 1) * 32],
                             [(Wa[:, kc, j * 128:(j + 1) * 128], xn[:, kc, T - 32:T]) for kc in range(8)],
                             R=[xn_t, st])
                        b.mm(pa_t[1], pa[1][:, cc * 32:(cc + 1) * 32],
                             [(Wg[:, kc, j * 128:(j + 1) * 128], xn[:, kc, T - 32:T]) for kc in range(8)],
                             R=[xn_t, st])
                    if hp == 1 and DBG >= 7:
                        b.op("act", lambda h: h.activation(sgt[:, :], pa[1][:, 0:128], AF.Sigmoid),
                             R=[pa_t[1]], W=[sgt_t])
                        b.op("dve", lambda h: h.tensor_tensor(xch_sb[:, 512:640], pa[0][:, 0:128], sgt[:, :], ALU.mult),
                             R=[pa_t[0], sgt_t], W=[xch_sb_t])
                step([(0, 2048, w_in[l][:, :, hp * 256:hp * 256 + 256]),
                      (2048, 4096, w_in[l][:, :, 512 + hp * 256:512 + hp * 256 + 256])], do_tail_piece)

            def fin(s, st):
                b.op("act", lambda h: h.copy(xch_sb[:, 0:512], pS1[:, :]), R=[pS1_t], W=[xch_sb_t])
                b.dma("sp", xch_out_d, xch_sb[:, :], R=[xch_sb_t])
            step(None, fin)

    def emit_P2_prologue(l):
        def pro(s, st):
            for r in range(8):
                x_ = xr[r % 2]
                x_t = xr_t[r % 2]
                b.dma("sp", x_[:, :], xch_in_d[:, r, :], W=[x_t])
                for hh in range(4):
                    cf = cmisc[:, CM_COEF + r * 4 + hh: CM_COEF + r * 4 + hh + 1]
                    src = x_[:, hh * 128:(hh + 1) * 128]
                    dst = S[:, hh * 128:(hh + 1) * 128]
                    if r == 0:
                        b.op("dve", lambda h: h.tensor_scalar(dst, src, cf, None, ALU.mult),
                             R=[x_t], W=[S_t])
                    else:
                        b.op("dve", lambda h: h.scalar_tensor_tensor(dst, src, cf, dst, ALU.mult, ALU.add),
                             R=[x_t], W=[S_t])
                sl = cmisc[:, CM_SEL + r: CM_SEL + r + 1]
                src = x_[:, 512:640].rearrange("p (c n) -> p c n", c=4)[:, :, 2:32]
                dst = hg[:, :, 0:30]
                if r == 0:
                    b.op("dve", lambda h: h.tensor_scalar(dst, src, sl, None, ALU.mult),
                         R=[x_t], W=[halo_t])
                else:
                    b.op("dve", lambda h: h.scalar_tensor_tensor(dst, src, sl, dst, ALU.mult, ALU.add),
                         R=[x_t], W=[halo_t])
            b.op("act", lambda h: h.copy(Sb_[:, :], S[:, :]), R=[S_t], W=[Sb_t])
        step(None, pro)

    def emit_P2(l, t, Xap, X_tl):
        g1 = l * PL + 0
        g2 = l * PL + 8
        cw0 = l * PL + 16
        cb0 = l * PL + 140
        lg0 = l * PL + 144
        lb0 = l * PL + 148
        ng0 = l * PL + 152
        kvb = kvt[t % 2]
        kvb_t = kvt_t[t % 2]

        def pre(s, st):
            b.dma("sp", kvb[:, :, :, :], kv_in_d[:, t * 4:(t + 1) * 4, :, :], W=[kvb_t])
            norm_tile(Xap, X_tl, g1, xn, xn_t)
        step(None, pre)

        for hp in range(2):
            def do_glu(s, st, hp=hp):
                Wa = s[:, 0:2048].rearrange("p (a n) -> p a n", a=8)
                Wg = s[:, 2048:4096].rearrange("p (a n) -> p a n", a=8)
                for j in range(2):
                    cc = hp * 2 + j
                    p1, p1_t = next_pa()
                    p2, p2_t = next_pa()
                    b.mm(p1_t, p1[:, :], [(Wa[:, kc, j * 128:(j + 1) * 128], xn[:, kc, :]) for kc in range(8)],
                         R=[xn_t, st])
                    b.mm(p2_t, p2[:, :], [(Wg[:, kc, j * 128:(j + 1) * 128], xn[:, kc, :]) for kc in range(8)],
                         R=[xn_t, st])
                    i = cc % 2
                    b.op("act", lambda h: h.activation(sig[i][:, :], p2[:, :], AF.Sigmoid), R=[p2_t], W=[sig_t[i]])
                    b.op("dve", lambda h: h.tensor_tensor(hg[:, cc, 30:30 + T], p1[:, :], sig[i][:, :], ALU.mult),
                         R=[p1_t, sig_t[i]], W=[hg_t[cc]])
            step([(0, 2048, w_in[l][:, :, hp * 256:hp * 256 + 256]),
                  (2048, 4096, w_in[l][:, :, 512 + hp * 256:512 + hp * 256 + 256])], do_glu)

        def do_conv(s, st):
            for cc in range(4):
                a_ = acc[:, cc, :]
                b.op("dve", lambda h: h.tensor_scalar(
                    a_, hg[:, cc, 0:T], params[:, cw0 + cc * CK: cw0 + cc * CK + 1],
                    params[:, cb0 + cc: cb0 + cc + 1], ALU.mult, ALU.add),
                    R=[hg_t[cc], halo_t], W=[acc_t[cc]])
                for j in range(1, CK):
                    b.op("dve", lambda h: h.scalar_tensor_tensor(
                        a_, hg[:, cc, j:j + T], params[:, cw0 + cc * CK + j: cw0 + cc * CK + j + 1], a_,
                        ALU.mult, ALU.add), R=[hg_t[cc], halo_t], W=[acc_t[cc]])
            b.op("dve", lambda h: h.tensor_copy(hg[:, :, 0:30], hg[:, :, T:T + 30]),
                 R=hg_t, W=[halo_t])
            pm, pm_t = next_pa()
            for cc in range(4):
                i = cc % 2
                b.op("act", lambda h: h.copy(abf[i][:, :], acc[:, cc, :]), R=[acc_t[cc]], W=[abf_t[i]])
                b.op("act", lambda h: h.activation(asq[i][:, :], acc[:, cc, :], AF.Square), R=[acc_t[cc]], W=[asq_t[i]])
                b.mm(pm_t, pm[:, :], [(ones_c[:, :], abf[i][:, :])], R=[abf_t[i]], start=(cc == 0), stop=(cc == 3))
                b.mm(pst_t, pst[:, :], [(ones_c[:, :], asq[i][:, :])], R=[asq_t[i]], start=(cc == 0), stop=(cc == 3))
            b.op("act", lambda h: h.copy(mean[:, :], pm[:, :]), R=[pm_t], W=[mean_t])
            b.op("dve", lambda h: h.tensor_tensor(lnt[0][:, :], mean[:, :], mean[:, :], ALU.mult),
                 R=[mean_t], W=[lnt_t[0]])
            b.op("dve", lambda h: h.tensor_tensor(lnt[1][:, :], pst[:, :], lnt[0][:, :], ALU.subtract),
                 R=[pst_t, lnt_t[0]], W=[lnt_t[1]])
            rsqrt(rstdc[:, :], rstdc_t, lnt[1][:, :], lnt_t[1])
            for cc in range(4):
                b.op("dve", lambda h: h.tensor_tensor(lnt[0][:, :], acc[:, cc, :], mean[:, :], ALU.subtract),
                     R=[acc_t[cc], mean_t], W=[lnt_t[0]])
                b.op("dve", lambda h: h.tensor_tensor(lnt[1][:, :], lnt[0][:, :], rstdc[:, :], ALU.mult),
                     R=[lnt_t[0], rstdc_t], W=[lnt_t[1]])
                b.op("dve", lambda h: h.tensor_scalar(
                    lnt[0][:, :], lnt[1][:, :], params[:, lg0 + cc: lg0 + cc + 1],
                    params[:, lb0 + cc: lb0 + cc + 1], ALU.mult, ALU.add), R=[lnt_t[1]], W=[lnt_t[0]])
                i = cc % 2
                b.op("act", lambda h: h.activation(sig[i][:, :], lnt[0][:, :], AF.Sigmoid),
                     R=[lnt_t[0]], W=[sig_t[i]])
                b.op("dve", lambda h: h.tensor_tensor(Abuf[:, cc, :], lnt[0][:, :], sig[i][:, :], ALU.mult),
                     R=[lnt_t[0], sig_t[i]], W=[A_t[cc]])
        step(None, do_conv)

        def do_q(s, st):
            Wv = s[:, :].rearrange("p (a n) -> p a n", a=8)
            for blk in range(4):
                p, p_t = next_pa()
                b.mm(p_t, p[:, :], [(xn[:, kc, blk * 128:(blk + 1) * 128], Wv[:, kc, :]) for kc in range(8)],
                     R=[xn_t, st])
                rotary(p[:, :], p_t, t * 4 + blk, qtok[:, blk, :], qtok_t[blk])
        step(win_piece(l, 1024), do_q)

        def do_g(s, st):
            Wv = s[:, :].rearrange("p (a n) -> p a n", a=8)
            for hh in range(4):
                p, p_t = next_pa()
                b.mm(p_t, p[:, :], [(Wv[:, kc, hh * 128:(hh + 1) * 128], xn[:, kc, :]) for kc in range(8)],
                     R=[xn_t, st])
                i = hh % 2
                b.op("act", lambda h: h.activation(sig[i][:, :], p[:, :], AF.Sigmoid), R=[p_t], W=[sig_t[i]])
                b.op("dve", lambda h: h.scalar_tensor_tensor(
                    sgn[:, hh, :], p[:, :], params[:, ng0 + hh: ng0 + hh + 1], sig[i][:, :], ALU.mult, ALU.mult),
                    R=[p_t, sig_t[i]], W=[sgn_t[hh]])
        step(win_piece(l, 2560), do_g)

        def do_ret(s, st):
            gam = [1.0 - 2.0 ** (-5 - hh) for hh in range(4)]
            for blk in range(4):
                kb = kvb[:, blk, 0, :]
                vb = kvb[:, blk, 1, :]
                for hh in range(4):
                    b.tr(ptr_t[0], ptr[0][:, hh * 128:(hh + 1) * 128], qtok[:, blk, hh * 128:(hh + 1) * 128],
                         ident[:, :], R=[qtok_t[blk]])
                b.op("act", lambda h: h.copy(qT[:, :], ptr[0][:, :]), R=[ptr_t[0]], W=[qT_t])
                b.op("dve", lambda h: h.tensor_tensor(qdT[:, :], ptr[0][:, :], DTAB, ALU.mult),
                     R=[ptr_t[0]], W=[qdT_t])
                for hh in range(4):
                    b.tr(ptr_t[1], ptr[1][:, hh * 128:(hh + 1) * 128], kb[:, hh * 128:(hh + 1) * 128],
                         ident[:, :], R=[kvb_t])
                b.op("act", lambda h: h.copy(kT[:, :], ptr[1][:, :]), R=[ptr_t[1]], W=[kT_t])
                p, p_t = next_pa()
                for hh in range(4):
                    c = slice(hh * 128, (hh + 1) * 128)
                    b.mm(p_t, p[:, c], [(kT[:, c], qT[:, c])], R=[kT_t, qT_t])
                b.op("dve", lambda h: h.tensor_tensor(PTm[:, :], p[:, :], MT, ALU.mult), R=[p_t], W=[PTm_t])
                b.op("dve", lambda h: h.tensor_tensor(
                    vw2[:, :].rearrange("p (h e) -> p h e", h=4), vb.rearrange("p (h e) -> p h e", h=4),
                    _bc(W2TAB, [128, 4, 128], (2,)), ALU.mult), R=[kvb_t], W=[vw2_t])
                for hh in range(4):
                    c = slice(hh * 128, (hh + 1) * 128)
                    b.mm(po_t, po[:, c], [(vb[:, c], PTm[:, c]), (Sb_[:, c], qdT[:, c])],
                         R=[kvb_t, PTm_t, Sb_t, qdT_t])
                for hh in range(4):
                    c = slice(hh * 128, (hh + 1) * 128)
                    b.mm(pS_t, pS[:, c], [(kb[:, c], vw2[:, c])], R=[kvb_t, vw2_t])
                for hh in range(4):
                    c = slice(hh * 128, (hh + 1) * 128)
                    b.op("dve", lambda h: h.scalar_tensor_tensor(
                        S[:, c], S[:, c], float(gam[hh] ** 128), pS[:, c], ALU.mult, ALU.add),
                        R=[pS_t], W=[S_t])
                b.op("act", lambda h: h.copy(Sb_[:, :], S[:, :]), R=[S_t], W=[Sb_t])
                b.op("act", lambda h: h.activation(osq[:, :], po[:, :], AF.Square), R=[po_t], W=[osq_t])
                b.mm(pst_t, pst[:, :], [(ones_h[:, :], osq[:, :])], R=[osq_t])
                rsqrt(rstdo[:, :], rstdo_t, pst[:, :], pst_t)
                b.op("dve", lambda h: h.tensor_tensor(bt[:, :], po[:, :], rstdo[:, :], ALU.mult),
                     R=[po_t, rstdo_t], W=[bt_t])
                b.op("dve", lambda h: h.tensor_tensor(
                    Bbuf[:, :, blk * 128:(blk + 1) * 128], bt[:, :].rearrange("p (h i) -> p h i", h=4),
                    sgn[:, :, blk * 128:(blk + 1) * 128], ALU.mult), R=[bt_t] + sgn_t, W=[Bb_t])
        step(None, do_ret)

        for half in range(2):
            def do_out(s, st, half=half):
                Wv = s[:, :].rearrange("p (a n) -> p a n", a=8)
                for j in range(4):
                    oc = half * 4 + j
                    p, p_t = next_pa()
                    pairs = []
                    for kc in range(8):
                        rhs = Abuf[:, kc, :] if kc < 4 else Bbuf[:, kc - 4, :]
                        pairs.append((Wv[:, kc, j * 128:(j + 1) * 128], rhs))
                    b.mm(p_t, p[:, :], pairs, R=A_t + [Bb_t, st])
                    b.op("dve", lambda h: h.tensor_tensor(Xap[:, oc, :], Xap[:, oc, :], p[:, :], ALU.add),
                         R=[p_t], W=[X_tl])
            step([(0, 4096, w_out[l][:, :, half * 512:(half + 1) * 512])], do_out)

        step(None, lambda s, st: norm_tile(Xap, X_tl, g2, xn, xn_t))
        for grp in range(2):
            for pc in range(4):
                def do_f1(s, st, grp=grp, pc=pc):
                    Wv = s[:, :].rearrange("p (a n) -> p a n", a=8)
                    for j in range(4):
                        hc = pc * 4 + j
                        p, p_t = next_pa()
                        b.mm(p_t, p[:, :], [(Wv[:, kc, j * 128:(j + 1) * 128], xn[:, kc, :]) for kc in range(8)],
                             R=[xn_t, st])
                        i = hc % 2
                        b.op("act", lambda h: h.activation(relu_s[i][:, :], p[:, :], AF.Relu), R=[p_t], W=[relu_t[i]])
                        b.op("act", lambda h: h.activation(H[:, hc, :], relu_s[i][:, :], AF.Square),
                             R=[relu_t[i]], W=[H_t[hc]])
                c0 = (grp * 4 + pc) * 512
                step([(0, 4096, w_ff1[l][:, :, c0:c0 + 512])], do_f1)
            for pc in range(4):
                def do_f2(s, st, grp=grp, pc=pc):
                    Wv = s[:, :].rearrange("p (a n) -> p a n", a=16)
                    for j in range(2):
                        oc = pc * 2 + j
                        p, p_t = next_pa()
                        b.mm(p_t, p[:, :], [(Wv[:, k2, j * 128:(j + 1) * 128], H[:, k2, :]) for k2 in range(16)],
                             R=H_t + [st])
                        b.op("dve", lambda h: h.tensor_tensor(Xap[:, oc, :], Xap[:, oc, :], p[:, :], ALU.add),
                             R=[p_t], W=[X_tl])
                step([(0, 4096, w_ff2[l][:, grp * 16:(grp + 1) * 16, pc * 256:(pc + 1) * 256])], do_f2)

    for p, l in phases:
        if p == "P2":
            emit_P2_prologue(l)
    for t in range(NT):
        Xap = Xb[t % 2]
        X_tl = Xt_[t % 2]

        def ldx(s, st, t=t, Xap=Xap, X_tl=X_tl):
            b.dma("sp", Xap[:, :, :], xT_d[:, :, t * T:(t + 1) * T], W=[X_tl])
        step(None, ldx)
        for p, l in phases:
            if p == "P1":
                emit_P1(l, t, Xap, X_tl)
            else:
                emit_P2(l, t, Xap, X_tl)
        if final:
            def fin_norm(s, st, t=t, Xap=Xap, X_tl=X_tl):
                norm_tile(Xap, X_tl, 2 * PL, yout, yout_t)
                b.dma("sp", xT_out_d[:, :, t * T:(t + 1) * T], yout[:, :, :], R=[yout_t])
            step(None, fin_norm)
        elif store_x:
            def stx(s, st, t=t, Xap=Xap, X_tl=X_tl):
                b.dma("sp", xT_out_d[:, :, t * T:(t + 1) * T], Xap[:, :, :], R=[X_tl])
            step(None, stx)
    run_steps()
    fin_deps = []
    for tl in Tl.all:
        fin_deps.append(tl.w)
        fin_deps.extend(tl.r)
    b._waits("sp", fin_deps)
    return nc


def _fm(a, kchunks):
    K, N = a.shape
    return np.ascontiguousarray(a.reshape(kchunks, 128, N).transpose(1, 0, 2))


def _consts(core):
    qt = core % 4
    gam = np.array([1.0 - 2.0 ** (-5 - h) for h in range(4)], np.float64)
    s = 128.0 ** -0.5
    inv_freq = (10000.0 ** (-np.arange(0, 128, 2, dtype=np.float32) / np.float32(128))).astype(np.float32)
    pos = (qt * TOK + np.arange(TOK, dtype=np.float32)).astype(np.float32)
    ang = (pos[:, None] * inv_freq[None, :]).astype(np.float32)
    c = np.cos(ang).astype(np.float32).reshape(NB, 128, 64).transpose(1, 0, 2)
    sn = np.sin(ang).astype(np.float32).reshape(NB, 128, 64).transpose(1, 0, 2)
    cm = np.zeros((128, CM_N), np.float32)
    i = np.arange(128)
    ii, jj = np.meshgrid(i, i, indexing="ij")
    for h in range(4):
        same = (ii // 64) == (jj // 64)
        causal = (ii // 64 == 1) & (jj // 64 == 0)
        M = np.where(same, gam[h] ** np.abs(ii - jj), np.where(causal, gam[h] ** np.maximum(ii - jj, 0), 0.0))
        cm[:, CM_MT + h * 128: CM_MT + (h + 1) * 128] = (s * M.T).astype(np.float32)
        cm[:, CM_DT + h * 128: CM_DT + (h + 1) * 128] = (gam[h] ** (i + 1.0))[None, :]
        cm[:, CM_W2 + h] = s * gam[h] ** (127.0 - i)
        for gb in range(NB):
            cm[:, CM_WT + gb * 4 + h] = s * gam[h] ** (2047.0 - (gb * 128 + i))
        for r in range(8):
            if r // 4 == core // 4 and r < core:
                cm[:, CM_COEF + r * 4 + h] = gam[h] ** (2048.0 * (core - 1 - r))
    cm[:, CM_ID:CM_ID + 128] = np.eye(128, dtype=np.float32)
    if qt > 0:
        cm[:, CM_SEL + core - 1] = 1.0
    return np.ascontiguousarray(c), np.ascontiguousarray(sn), cm


def _params(norm1_g, conv_w, conv_b, conv_ln_g, conv_ln_b, ret_norm_g, norm2_g, final_g):
    P = np.zeros((128, NPAR), np.float32)
    for l in range(NL):
        o = l * PL
        P[:, o:o + 8] = norm1_g[l].reshape(8, 128).T
        P[:, o + 8:o + 16] = norm2_g[l].reshape(8, 128).T
        cw = conv_w[l].reshape(CK, 4, 128)
        P[:, o + 16:o + 16 + 4 * CK] = cw.transpose(2, 1, 0).reshape(128, 4 * CK)
        P[:, o + 140:o + 144] = conv_b[l].reshape(4, 128).T
        P[:, o + 144:o + 148] = conv_ln_g[l].reshape(4, 128).T
        P[:, o + 148:o + 152] = conv_ln_b[l].reshape(4, 128).T
        P[:, o + 152:o + 156] = ret_norm_g[l].reshape(4, 128).T
    P[:, 2 * PL:2 * PL + 8] = final_g.reshape(8, 128).T
    return P


_NC_CACHE = {}


def _get_nc(key, *a, **k):
    if key not in _NC_CACHE:
        _NC_CACHE[key] = build(*a, **k)
    return _NC_CACHE[key]


def kernel(x, norm1_g, w_in, conv_w, conv_b, conv_ln_g, conv_ln_b, ret_norm_g,
           w_out, norm2_g, w_ff1, w_ff2, final_g):
    f = lambda a: np.asarray(a, dtype=np.float32)
    x, norm1_g, w_in, conv_w, conv_b = f(x), f(norm1_g), f(w_in), f(conv_w), f(conv_b)
    conv_ln_g, conv_ln_b, ret_norm_g, w_out = f(conv_ln_g), f(conv_ln_b), f(ret_norm_g), f(w_out)
    norm2_g, w_ff1, w_ff2, final_g = f(norm2_g), f(w_ff1), f(w_ff2), f(final_g)
    ncores = 8
    cores = list(range(ncores))
    P = _params(norm1_g, conv_w, conv_b, conv_ln_g, conv_ln_b, ret_norm_g, norm2_g, final_g)
    Win = [_fm(w_in[l], 8) for l in range(NL)]
    Wout = [_fm(w_out[l], 8) for l in range(NL)]
    Wf1 = [_fm(w_ff1[l], 8) for l in range(NL)]
    Wf2 = [_fm(w_ff2[l], 32) for l in range(NL)]
    cst = [_consts(c) for c in cores]
    xT = []
    for c in cores:
        bi, qt = c // 4, c % 4
        xs = x[bi, qt * TOK:(qt + 1) * TOK, :]
        xT.append(_fm(np.ascontiguousarray(xs.T), 8))

    def base(c):
        return {"params": P, "ctab": cst[c][0], "stab": cst[c][1], "cmisc": cst[c][2]}

    nc1 = _get_nc("L1", [("P1", 0)])
    maps = [dict(base(c), w_in0=Win[0], xT=xT[c]) for c in cores]
    r1 = run_bass_kernel_spmd(nc1, maps, core_ids=cores).results
    xch = np.ascontiguousarray(np.stack([r1[c]["xch_out"] for c in cores], 0).transpose(1, 0, 2))
    nc2 = _get_nc("L2", [("P2", 0), ("P1", 1)], store_x=True)
    maps = [dict(base(c), w_in0=Win[0], w_in1=Win[1], w_out0=Wout[0], w_ff10=Wf1[0], w_ff20=Wf2[0],
                 xT=xT[c], xch_in=xch, kv_in=r1[c]["kv_out"]) for c in cores]
    r2 = run_bass_kernel_spmd(nc2, maps, core_ids=cores).results
    xch = np.ascontiguousarray(np.stack([r2[c]["xch_out"] for c in cores], 0).transpose(1, 0, 2))
    nc3 = _get_nc("L3", [("P2", 1)], final=True)
    maps = [dict(base(c), w_in1=Win[1], w_out1=Wout[1], w_ff11=Wf1[1], w_ff21=Wf2[1],
                 xT=r2[c]["xT_out"], xch_in=xch, kv_in=r2[c]["kv_out"]) for c in cores]
    r3 = run_bass_kernel_spmd(nc3, maps, core_ids=cores).results
    out = np.empty((2, SEQ, D), np.float32)
    for c in cores:
        bi, qt = c // 4, c % 4
        y = r3[c]["xT_out"]
        out[bi, qt * TOK:(qt + 1) * TOK, :] = y.transpose(1, 0, 2).reshape(D, TOK).T
    return out
```

```python
import numpy as np
import ml_dtypes
import concourse.bass as bass
import concourse.mybir as mybir
from concourse.bass_utils import run_bass_kernel_spmd

F32 = mybir.dt.float32
BF16 = mybir.dt.bfloat16
ALU = mybir.AluOpType
AF = mybir.ActivationFunctionType

NL = 2
D = 1024
SEQ = 8192
TOK = 2048
T = 512
NT = TOK // T
NB = TOK // 128
EPS = 1e-6
CK = 31
PL = 156
NPAR = 2 * PL + 8
NSLOT = 3
DBG = 99
CM_MT = 0
CM_DT = 512
CM_W2 = 1024
CM_WT = 1028
CM_ID = 1092
CM_COEF = 1220
CM_SEL = 1252
CM_N = 1260
XW = 640


class Tl:
    all = []

    def __init__(self, psum=False):
        Tl.all.append(self)
        self.psum = psum
        self.w = None
        self.r = []
        self.sem = None
        self.cnt = 0


class B:
    def __init__(self, nc):
        self.nc = nc
        self.eng = {}
        for name, h in (("pe", nc.tensor), ("dve", nc.vector), ("act", nc.scalar),
                        ("pool", nc.gpsimd), ("sp", nc.sync)):
            sem = nc.alloc_semaphore("s_" + name)
            self.eng[name] = dict(h=h, sem=sem, cnt=0, seen={})
        self.nsem = 0

    def _waits(self, e, deps):
        E = self.eng[e]
        best = {}
        for d in deps:
            if d is None:
                continue
            sem, val = d
            if e == "pe" and sem is self.eng["pe"]["sem"]:
                continue
            k = id(sem)
            if k not in best or best[k][1] < val:
                best[k] = (sem, val)
        for k, (sem, val) in best.items():
            if E["seen"].get(k, 0) >= val:
                continue
            E["h"].wait_ge(sem, val)
            E["seen"][k] = val

    def _deps(self, R, W, deps, e=None):
        d = list(deps)
        for t in R:
            d.append(t.w)
            if t.psum:
                own = self.eng[e]["sem"]
                d.extend(tok for tok in t.r if tok[0] is not own)
        for t in W:
            d.append(t.w)
            d.extend(t.r)
        return d

    def _commit(self, tok, R, W):
        for t in R:
            t.r.append(tok)
        for t in W:
            t.w = tok
            t.r = []

    def op(self, e, fn, R=(), W=(), deps=()):
        E = self.eng[e]
        self._waits(e, self._deps(R, W, deps, e))
        ins = fn(E["h"])
        ins.then_inc(E["sem"], 1)
        E["cnt"] += 1
        tok = (E["sem"], E["cnt"])
        self._commit(tok, R, W)
        return tok

    def mm(self, Wt, out, pairs, R=(), start=True, stop=True, skip=False, deps=()):
        E = self.eng["pe"]
        self._waits("pe", self._deps(R, [Wt], deps, "pe"))
        n = len(pairs)
        ins = None
        for i, (l, r) in enumerate(pairs):
            kw = {}
            if skip:
                kw["skip_group_check"] = True
            ins = self.nc.tensor.matmul(out, l, r, start=(start and i == 0),
                                        stop=(stop and i == n - 1), **kw)
        ins.then_inc(E["sem"], 1)
        E["cnt"] += 1
        tok = (E["sem"], E["cnt"])
        self._commit(tok, R, [Wt])
        return tok

    def tr(self, Wt, out, in_, ident, R=()):
        E = self.eng["pe"]
        self._waits("pe", self._deps(R, [Wt], (), "pe"))
        ins = self.nc.tensor.transpose(out, in_, ident)
        ins.then_inc(E["sem"], 1)
        E["cnt"] += 1
        tok = (E["sem"], E["cnt"])
        self._commit(tok, R, [Wt])
        return tok

    def dma_multi(self, e, lst, t):
        E = self.eng[e]
        self._waits(e, self._deps((), [t], (), e))
        if t.sem is None:
            self.nsem += 1
            t.sem = self.nc.alloc_semaphore(f"d{self.nsem}")
        for out, in_ in lst:
            E["h"].dma_start(out=out, in_=in_).then_inc(t.sem, 16)
            t.cnt += 16
        tok = (t.sem, t.cnt)
        self._commit(tok, (), [t])
        return tok

    def dma(self, e, out, in_, R=(), W=(), deps=()):
        E = self.eng[e]
        self._waits(e, self._deps(R, W, deps, e))
        t = (list(W) + list(R))[0]
        if t.sem is None:
            self.nsem += 1
            t.sem = self.nc.alloc_semaphore(f"d{self.nsem}")
        E["h"].dma_start(out=out, in_=in_).then_inc(t.sem, 16)
        t.cnt += 16
        tok = (t.sem, t.cnt)
        self._commit(tok, R, W)
        return tok


def _bc(ap, shape, axes):
    for a in axes:
        ap = ap.unsqueeze(a)
    return ap.broadcast_to(list(shape))


def build(phases, store_x=False, final=False):
    nc = bass.Bass("TRN2", target_bir_lowering=False)
    Tl.all = []
    b = B(nc)
    lay = sorted({l for _, l in phases})
    has_p1 = [l for p, l in phases if p == "P1"]
    has_p2 = [l for p, l in phases if p == "P2"]

    def din(name, shape, dt=F32):
        return nc.dram_tensor(name, list(shape), dt, kind="ExternalInput").ap()

    def dout(name, shape, dt=F32):
        return nc.dram_tensor(name, list(shape), dt, kind="ExternalOutput").ap()

    w_in = {l: din(f"w_in{l}", [128, 8, 3072]) for l in lay}
    w_out = {l: din(f"w_out{l}", [128, 8, 1024]) for l in has_p2}
    w_ff1 = {l: din(f"w_ff1{l}", [128, 8, 4096]) for l in has_p2}
    w_ff2 = {l: din(f"w_ff2{l}", [128, 32, 1024]) for l in has_p2}
    params_d = din("params", [128, NPAR])
    ctab_d = din("ctab", [128, NB, 64])
    stab_d = din("stab", [128, NB, 64])
    cmisc_d = din("cmisc", [128, CM_N])
    xT_d = din("xT", [128, 8, TOK])
    if has_p2:
        xch_in_d = din("xch_in", [128, 8, XW])
        kv_in_d = din("kv_in", [128, NB, 2, 512], BF16)
    if has_p1:
        xch_out_d = dout("xch_out", [128, XW])
        kv_out_d = dout("kv_out", [128, NB, 2, 512], BF16)
    if store_x or final:
        xT_out_d = dout("xT_out", [128, 8, TOK])

    def sb(name, shape, dt=F32):
        return nc.alloc_sbuf_tensor(name, list(shape), dt)

    def ps(name, shape, dt=F32):
        return nc.alloc_psum_tensor(name, list(shape), dt)

    params = sb("params_sb", [128, NPAR]); params_t = Tl()
    ctab = sb("ctab_sb", [128, NB, 64]); ctab_t = Tl()
    stab = sb("stab_sb", [128, NB, 64]); stab_t = Tl()
    cmisc = sb("cmisc_sb", [128, CM_N]); cmisc_t = Tl()
    ident = sb("ident_sb", [128, 128], BF16); ident_t = Tl()
    ones_d = sb("ones_d", [128, 128], BF16)
    ones_c = sb("ones_c", [128, 128], BF16)
    ones_h = sb("ones_h", [128, 128], BF16)
    c_toks = [
        b.dma("sp", params[:, :], params_d, W=[params_t]),
        b.dma("sp", ctab[:, :, :], ctab_d, W=[ctab_t]),
        b.dma("sp", stab[:, :, :], stab_d, W=[stab_t]),
        b.dma("sp", cmisc[:, :], cmisc_d, W=[cmisc_t]),
        b.dma("pool", ident[:, :], cmisc_d[:, CM_ID:CM_ID + 128], W=[ident_t]),
    ]
    c_toks.append(b.op("dve", lambda h: h.memset(ones_d[:, :], 1.0 / 1024)))
    c_toks.append(b.op("dve", lambda h: h.memset(ones_c[:, :], 1.0 / 512)))
    c_toks.append(b.op("dve", lambda h: h.memset(ones_h[:, :], 1.0 / 128)))
    for e in ("pe", "dve", "act"):
        b._waits(e, c_toks)
    MT = cmisc[:, CM_MT:CM_MT + 512]
    DTAB = cmisc[:, CM_DT:CM_DT + 512]
    W2TAB = cmisc[:, CM_W2:CM_W2 + 4]

    def pcol(l, off):
        return params[:, l * PL + off: l * PL + off + 1]

    Xb = [sb(f"X{i}", [128, 8, T]) for i in range(2)]
    Xt_ = [Tl() for _ in range(2)]
    xn = sb("xn", [128, 8, T], BF16); xn_t = Tl()
    xnF = sb("xnF", [128, 8, T], BF16); xnF_t = Tl()
    sq = [sb(f"sq{i}", [128, T], BF16) for i in range(2)]; sq_t = [Tl(), Tl()]
    rstd = sb("rstd", [128, T]); rstd_t = Tl()
    rt1 = sb("rt1", [128, 512]); rt1_t = Tl()
    rt2 = sb("rt2", [128, 512]); rt2_t = Tl()
    slots = [sb(f"wslot{i}", [128, 4096], BF16) for i in range(NSLOT)]
    slot_t = [Tl() for _ in range(NSLOT)]
    pa = [ps(f"pa{i}", [128, 512]) for i in range(3)]; pa_t = [Tl(True) for _ in range(3)]
    pS1 = ps("pS1", [128, 512]); pS1_t = Tl(True)
    pst = ps("pst", [128, 512]); pst_t = Tl(True)
    pS = ps("pS", [128, 512]); pS_t = Tl(True)
    po = ps("po", [128, 512]); po_t = Tl(True)
    ptr_all = ps("ptr", [128, 1024], BF16)
    ptr = [ptr_all[:, 0:512], ptr_all[:, 512:1024]]; ptr_t = [Tl(True)] * 2
    pa_rr = [0]

    def next_pa():
        i = pa_rr[0] % 3
        pa_rr[0] += 1
        return pa[i], pa_t[i]

    if has_p1:
        ktok = sb("ktok", [128, 4, 512], BF16); ktok_t = [Tl() for _ in range(4)]
        vtok = [sb(f"vtok{i}", [128, 512], BF16) for i in range(2)]; vtok_t = [Tl(), Tl()]
        vW = [sb(f"vW{i}", [128, 512], BF16) for i in range(2)]; vW_t = [Tl(), Tl()]
        xch_sb = sb("xch_sb", [128, XW]); xch_sb_t = Tl()
        sgt = sb("sgt", [128, 128]); sgt_t = Tl()
    if has_p2:
        xr = [sb(f"xr{i}", [128, XW]) for i in range(2)]; xr_t = [Tl(), Tl()]
        hg = sb("hg", [128, 4, 30 + T], BF16); hg_t = [Tl() for _ in range(4)]; halo_t = Tl()
        dg = sb("dg", [128, CK, 128], BF16); dg_t = Tl()
        sig = [sb(f"sig{i}", [128, T]) for i in range(2)]; sig_t = [Tl(), Tl()]
        acc = sb("acc", [128, 4, T]); acc_t = [Tl() for _ in range(4)]
        abf = [sb(f"abf{i}", [128, T], BF16) for i in range(2)]; abf_t = [Tl(), Tl()]
        asq = [sb(f"asq{i}", [128, T], BF16) for i in range(2)]; asq_t = [Tl(), Tl()]
        mean = sb("mean", [128, T]); mean_t = Tl()
        rstdc = sb("rstdc", [128, T]); rstdc_t = Tl()
        lnt = [sb(f"lnt{i}", [128, T]) for i in range(2)]; lnt_t = [Tl(), Tl()]
        Abuf = sb("Abuf", [128, 4, T], BF16); A_t = [Tl() for _ in range(4)]
        Bbuf = sb("Bbuf", [128, 4, T], BF16); Bb_t = Tl()
        qtok = sb("qtok", [128, 4, 512], BF16); qtok_t = [Tl() for _ in range(4)]
        sgn = sb("sgn", [128, 4, T]); sgn_t = [Tl() for _ in range(4)]
        kvt = [sb("kvt0", [128, 4, 2, 512], BF16)] * 2; kvt_t = [Tl()] * 2
        qT = sb("qT", [128, 512], BF16); qT_t = Tl()
        qdT = sb("qdT", [128, 512], BF16); qdT_t = Tl()
        kT = sb("kT", [128, 512], BF16); kT_t = Tl()
        PTm = sb("PTm", [128, 512], BF16); PTm_t = Tl()
        vw2 = sb("vw2", [128, 512], BF16); vw2_t = Tl()
        S = sb("S", [128, 512]); S_t = Tl()
        Sb_ = sb("Sb", [128, 512], BF16); Sb_t = Tl()
        osq = sb("osq", [128, 512], BF16); osq_t = Tl()
        rstdo = sb("rstdo", [128, 512]); rstdo_t = Tl()
        bt = sb("bt", [128, 512]); bt_t = Tl()
        H = sb("H", [128, 16, T], BF16); H_t = [Tl() for _ in range(16)]
        relu_s = lnt; relu_t = lnt_t
    if final:
        yout = sb("yout", [128, 4, T]); yout_t = Tl()

    steps = []
    cur = [steps]

    def step(piece, fn):
        cur[0].append((piece, fn))

    def run_steps():
        pidx = [i for i, s in enumerate(steps) if s[0] is not None]
        issued = 0

        def issue(j):
            si = j % NSLOT
            lst = []
            for (c0, c1, src) in steps[pidx[j]][0]:
                dst = slots[si][:, c0:c1]
                if len(src.shape) == 3:
                    dst = dst.rearrange("p (a n) -> p a n", a=src.shape[1])
                lst.append((dst, src))
            b.dma_multi("pool", lst, slot_t[si])

        k = 0
        for i, (piece, fn) in enumerate(steps):
            if piece is not None:
                while issued < len(pidx) and issued <= k + NSLOT - 1:
                    issue(issued)
                    issued += 1
                si = k % NSLOT
                fn(slots[si], slot_t[si])
                k += 1
            else:
                fn(None, None)

    rq = sb("rq", [128, 512]); rq_t = Tl()

    def rsqrt(dst, dst_t, src, src_t):
        b.op("act", lambda h: h.activation(rq[:, :], src, AF.Sqrt, bias=EPS), R=[src_t], W=[rq_t])
        b.op("dve", lambda h: h.reciprocal(dst, rq[:, :]), R=[rq_t], W=[dst_t])

    def norm_tile(Xap, X_tl, goff, out3, out_tl, half=None):
        if half in (None, 0):
            for kc in range(8):
                i = kc % 2
                b.op("act", lambda h: h.activation(sq[i][:, :], Xap[:, kc, :], AF.Square),
                     R=[X_tl], W=[sq_t[i]])
                b.mm(pst_t, pst[:, :], [(ones_d[:, :], sq[i][:, :])], R=[sq_t[i]],
                     start=(kc == 0), stop=(kc == 7))
            rsqrt(rstd[:, :], rstd_t, pst[:, :], pst_t)
        rng = range(8) if half is None else range(half * 4, half * 4 + 4)
        for kc in rng:
            ko = kc if half is None else kc - half * 4
            b.op("dve", lambda h: h.scalar_tensor_tensor(
                out3[:, ko, :], Xap[:, kc, :], params[:, goff + kc: goff + kc + 1], rstd[:, :],
                ALU.mult, ALU.mult), R=[X_tl, rstd_t], W=[out_tl])

    def rotary(psrc, psrc_t, blk_g, dst, dst_t):
        x4 = psrc.rearrange("p (h two d) -> p h two d", h=4, two=2)
        c = ctab[:, blk_g, :]
        s = stab[:, blk_g, :]
        t1v = rt1[:, :].rearrange("p (h two d) -> p h two d", h=4, two=2)
        t2v = rt2[:, :].rearrange("p (h two d) -> p h two d", h=4, two=2)
        b.op("dve", lambda h: h.tensor_tensor(t1v, x4, _bc(c, [128, 4, 2, 64], (1, 1)), ALU.mult),
             R=[psrc_t], W=[rt1_t])
        b.op("dve", lambda h: h.scalar_tensor_tensor(
            t2v[:, :, 0, :], x4[:, :, 1, :], -1.0, _bc(s, [128, 4, 64], (1,)), ALU.mult, ALU.mult),
            R=[psrc_t], W=[rt2_t])
        b.op("dve", lambda h: h.tensor_tensor(
            t2v[:, :, 1, :], x4[:, :, 0, :], _bc(s, [128, 4, 64], (1,)), ALU.mult),
            R=[psrc_t], W=[rt2_t])
        b.op("dve", lambda h: h.tensor_tensor(dst, rt1[:, :], rt2[:, :], ALU.add),
             R=[rt1_t, rt2_t], W=[dst_t])

    def win_piece(l, c0, n=512):
        return [(0, 8 * n, w_in[l][:, :, c0:c0 + n])]

    def emit_P1(l, t, Xap, X_tl):
        g1 = l * PL + 0
        if DBG < 2:
            return
        step(None, lambda s, st: norm_tile(Xap, X_tl, g1, xn, xn_t))
        if DBG < 3:
            return

        def do_k(s, st):
            Wv = s[:, :].rearrange("p (a n) -> p a n", a=8)
            for blk in range(4):
                p, p_t = next_pa()
                b.mm(p_t, p[:, :], [(xn[:, kc, blk * 128:(blk + 1) * 128], Wv[:, kc, :]) for kc in range(8)],
                     R=[xn_t, st])
                rotary(p[:, :], p_t, t * 4 + blk, ktok[:, blk, :], ktok_t[blk])
                b.dma("sp", kv_out_d[:, t * 4 + blk, 0, :], ktok[:, blk, :], R=[ktok_t[blk]])
        step(win_piece(l, 1536), do_k)
        if DBG < 4:
            return

        def do_v(s, st):
            Wv = s[:, :].rearrange("p (a n) -> p a n", a=8)
            for blk in range(4):
                gb = t * 4 + blk
                i = blk % 2
                p, p_t = next_pa()
                b.mm(p_t, p[:, :], [(xn[:, kc, blk * 128:(blk + 1) * 128], Wv[:, kc, :]) for kc in range(8)],
                     R=[xn_t, st])
                b.op("act", lambda h: h.copy(vtok[i][:, :], p[:, :]), R=[p_t], W=[vtok_t[i]])
                b.dma("sp", kv_out_d[:, gb, 1, :], vtok[i][:, :], R=[vtok_t[i]])
                wt = cmisc[:, CM_WT + gb * 4: CM_WT + gb * 4 + 4]
                b.op("dve", lambda h: h.tensor_tensor(
                    vW[i][:, :].rearrange("p (h e) -> p h e", h=4), p[:, :].rearrange("p (h e) -> p h e", h=4),
                    _bc(wt, [128, 4, 128], (2,)), ALU.mult), R=[p_t], W=[vW_t[i]])
                for hh in range(4):
                    b.mm(pS1_t, pS1[:, hh * 128:(hh + 1) * 128],
                         [(ktok[:, blk, hh * 128:(hh + 1) * 128], vW[i][:, hh * 128:(hh + 1) * 128])],
                         R=[ktok_t[blk], vW_t[i]], start=(gb == 0 and hh == 0), stop=(gb == NB - 1 and hh == 3),
                         skip=True)
        step(win_piece(l, 2048), do_v)
        if DBG < 5:
            return

        if t == NT - 1:
            for hp in range(2 if DBG >= 6 else 0):
                def do_tail_piece(s, st, hp=hp):
                    Wa = s[:, 0:2048].rearrange("p (a n) -> p a n", a=8)
                    Wg = s[:, 2048:4096].rearrange("p (a n) -> p a n", a=8)
                    for j in range(2):
                        cc = hp * 2 + j
                        b.mm(pa_t[0], pa[0][:, cc * 32:(cc + 1) * 32],
                             [(Wa[:, kc, j * 128:(j + 1) * 128], xn[:, kc, T - 32:T]) for kc in range(8)],
                             R=[xn_t, st])
                        b.mm(pa_t[1], pa[1][:, cc * 32:(cc + 1) * 32],
                             [(Wg[:, kc, j * 128:(j + 1) * 128], xn[:, kc, T - 32:T]) for kc in range(8)],
                             R=[xn_t, st])
                    if hp == 1 and DBG >= 7:
                        b.op("act", lambda h: h.activation(sgt[:, :], pa[1][:, 0:128], AF.Sigmoid),
                             R=[pa_t[1]], W=[sgt_t])
                        b.op("dve", lambda h: h.tensor_tensor(xch_sb[:, 512:640], pa[0][:, 0:128], sgt[:, :], ALU.mult),
                             R=[pa_t[0], sgt_t], W=[xch_sb_t])
                step([(0, 2048, w_in[l][:, :, hp * 256:hp * 256 + 256]),
                      (2048, 4096, w_in[l][:, :, 512 + hp * 256:512 + hp * 256 + 256])], do_tail_piece)

            def fin(s, st):
                b.op("act", lambda h: h.copy(xch_sb[:, 0:512], pS1[:, :]), R=[pS1_t], W=[xch_sb_t])
                b.dma("sp", xch_out_d, xch_sb[:, :], R=[xch_sb_t])
            step(None, fin)

    def emit_P2_prologue(l):
        def pro(s, st):
            for r in range(8):
                x_ = xr[r % 2]
                x_t = xr_t[r % 2]
                b.dma("sp", x_[:, :], xch_in_d[:, r, :], W=[x_t])
                for hh in range(4):
                    cf = cmisc[:, CM_COEF + r * 4 + hh: CM_COEF + r * 4 + hh + 1]
                    src = x_[:, hh * 128:(hh + 1) * 128]
                    dst = S[:, hh * 128:(hh + 1) * 128]
                    if r == 0:
                        b.op("dve", lambda h: h.tensor_scalar(dst, src, cf, None, ALU.mult),
                             R=[x_t], W=[S_t])
                    else:
                        b.op("dve", lambda h: h.scalar_tensor_tensor(dst, src, cf, dst, ALU.mult, ALU.add),
                             R=[x_t], W=[S_t])
                sl = cmisc[:, CM_SEL + r: CM_SEL + r + 1]
                src = x_[:, 512:640].rearrange("p (c n) -> p c n", c=4)[:, :, 2:32]
                dst = hg[:, :, 0:30]
                if r == 0:
                    b.op("dve", lambda h: h.tensor_scalar(dst, src, sl, None, ALU.mult),
                         R=[x_t], W=[halo_t])
                else:
                    b.op("dve", lambda h: h.scalar_tensor_tensor(dst, src, sl, dst, ALU.mult, ALU.add),
                         R=[x_t], W=[halo_t])
            b.op("act", lambda h: h.copy(Sb_[:, :], S[:, :]), R=[S_t], W=[Sb_t])
        step(None, pro)

    def emit_P2(l, t, Xap, X_tl, part):
        g1 = l * PL + 0
        g2 = l * PL + 8
        cw0 = l * PL + 16
        cb0 = l * PL + 140
        lg0 = l * PL + 144
        lb0 = l * PL + 148
        ng0 = l * PL + 152
        kvb = kvt[t % 2]
        kvb_t = kvt_t[t % 2]

        if part == "C":
            emit_P2_C(l, t, Xap, X_tl, g2)
            return

        def pre(s, st):
            b.dma("sp", kvb[:, :, :, :], kv_in_d[:, t * 4:(t + 1) * 4, :, :], W=[kvb_t])
            norm_tile(Xap, X_tl, g1, xn, xn_t)
        step(None, pre)

        for hp in range(2):
            def do_glu(s, st, hp=hp):
                Wa = s[:, 0:2048].rearrange("p (a n) -> p a n", a=8)
                Wg = s[:, 2048:4096].rearrange("p (a n) -> p a n", a=8)
                for j in range(2):
                    cc = hp * 2 + j
                    p1, p1_t = next_pa()
                    p2, p2_t = next_pa()
                    b.mm(p1_t, p1[:, :], [(Wa[:, kc, j * 128:(j + 1) * 128], xn[:, kc, :]) for kc in range(8)],
                         R=[xn_t, st])
                    b.mm(p2_t, p2[:, :], [(Wg[:, kc, j * 128:(j + 1) * 128], xn[:, kc, :]) for kc in range(8)],
                         R=[xn_t, st])
                    i = cc % 2
                    b.op("act", lambda h: h.activation(sig[i][:, :], p2[:, :], AF.Sigmoid), R=[p2_t], W=[sig_t[i]])
                    b.op("dve", lambda h: h.tensor_tensor(hg[:, cc, 30:30 + T], p1[:, :], sig[i][:, :], ALU.mult),
                         R=[p1_t, sig_t[i]], W=[hg_t[cc]])
            step([(0, 2048, w_in[l][:, :, hp * 256:hp * 256 + 256]),
                  (2048, 4096, w_in[l][:, :, 512 + hp * 256:512 + hp * 256 + 256])], do_glu)

        for cc in range(4):
            def do_conv_cc(s, st, cc=cc):
                wv = params[:, cw0 + cc * CK: cw0 + (cc + 1) * CK]
                b.op("dve", lambda h: h.tensor_tensor(
                    dg[:, :, :], _bc(ident[:, :], [128, CK, 128], (1,)), _bc(wv, [128, CK, 128], (2,)), ALU.mult),
                    W=[dg_t])
                p, p_t = next_pa()
                b.mm(p_t, p[:, :], [(dg[:, j, :], hg[:, cc, j:j + T]) for j in range(CK)],
                     R=[dg_t, hg_t[cc], halo_t])
                b.op("act", lambda h: h.activation(acc[:, cc, :], p[:, :], AF.Identity,
                                                   bias=params[:, cb0 + cc: cb0 + cc + 1]),
                     R=[p_t], W=[acc_t[cc]])
            step(None, do_conv_cc)

        def do_conv(s, st):
            b.op("dve", lambda h: h.tensor_copy(hg[:, :, 0:30], hg[:, :, T:T + 30]),
                 R=hg_t, W=[halo_t])
            pm, pm_t = next_pa()
            for cc in range(4):
                i = cc % 2
                b.op("act", lambda h: h.copy(abf[i][:, :], acc[:, cc, :]), R=[acc_t[cc]], W=[abf_t[i]])
                b.op("act", lambda h: h.activation(asq[i][:, :], acc[:, cc, :], AF.Square), R=[acc_t[cc]], W=[asq_t[i]])
                b.mm(pm_t, pm[:, :], [(ones_c[:, :], abf[i][:, :])], R=[abf_t[i]], start=(cc == 0), stop=(cc == 3))
                b.mm(pst_t, pst[:, :], [(ones_c[:, :], asq[i][:, :])], R=[asq_t[i]], start=(cc == 0), stop=(cc == 3))
            b.op("act", lambda h: h.copy(mean[:, :], pm[:, :]), R=[pm_t], W=[mean_t])
            b.op("dve", lambda h: h.tensor_tensor(lnt[0][:, :], mean[:, :], mean[:, :], ALU.mult),
                 R=[mean_t], W=[lnt_t[0]])
            b.op("dve", lambda h: h.tensor_tensor(lnt[1][:, :], pst[:, :], lnt[0][:, :], ALU.subtract),
                 R=[pst_t, lnt_t[0]], W=[lnt_t[1]])
            rsqrt(rstdc[:, :], rstdc_t, lnt[1][:, :], lnt_t[1])
            for cc in range(4):
                b.op("dve", lambda h: h.tensor_tensor(lnt[0][:, :], acc[:, cc, :], mean[:, :], ALU.subtract),
                     R=[acc_t[cc], mean_t], W=[lnt_t[0]])
                b.op("dve", lambda h: h.tensor_tensor(lnt[1][:, :], lnt[0][:, :], rstdc[:, :], ALU.mult),
                     R=[lnt_t[0], rstdc_t], W=[lnt_t[1]])
                b.op("dve", lambda h: h.tensor_scalar(
                    lnt[0][:, :], lnt[1][:, :], params[:, lg0 + cc: lg0 + cc + 1],
                    params[:, lb0 + cc: lb0 + cc + 1], ALU.mult, ALU.add), R=[lnt_t[1]], W=[lnt_t[0]])
                i = cc % 2
                b.op("act", lambda h: h.activation(sig[i][:, :], lnt[0][:, :], AF.Sigmoid),
                     R=[lnt_t[0]], W=[sig_t[i]])
                b.op("dve", lambda h: h.tensor_tensor(Abuf[:, cc, :], lnt[0][:, :], sig[i][:, :], ALU.mult),
                     R=[lnt_t[0], sig_t[i]], W=[A_t[cc]])
        step(None, do_conv)

        def do_q(s, st):
            Wv = s[:, :].rearrange("p (a n) -> p a n", a=8)
            for blk in range(4):
                p, p_t = next_pa()
                b.mm(p_t, p[:, :], [(xn[:, kc, blk * 128:(blk + 1) * 128], Wv[:, kc, :]) for kc in range(8)],
                     R=[xn_t, st])
                rotary(p[:, :], p_t, t * 4 + blk, qtok[:, blk, :], qtok_t[blk])
        step(win_piece(l, 1024), do_q)

        def do_g(s, st):
            Wv = s[:, :].rearrange("p (a n) -> p a n", a=8)
            for hh in range(4):
                p, p_t = next_pa()
                b.mm(p_t, p[:, :], [(Wv[:, kc, hh * 128:(hh + 1) * 128], xn[:, kc, :]) for kc in range(8)],
                     R=[xn_t, st])
                i = hh % 2
                b.op("act", lambda h: h.activation(sig[i][:, :], p[:, :], AF.Sigmoid), R=[p_t], W=[sig_t[i]])
                b.op("dve", lambda h: h.scalar_tensor_tensor(
                    sgn[:, hh, :], p[:, :], params[:, ng0 + hh: ng0 + hh + 1], sig[i][:, :], ALU.mult, ALU.mult),
                    R=[p_t, sig_t[i]], W=[sgn_t[hh]])
        step(win_piece(l, 2560), do_g)

        gam = [1.0 - 2.0 ** (-5 - hh) for hh in range(4)]
        for blk in range(4):
            def do_ret(s, st, blk=blk):
                kb = kvb[:, blk, 0, :]
                vb = kvb[:, blk, 1, :]
                for hh in range(4):
                    b.tr(ptr_t[0], ptr[0][:, hh * 128:(hh + 1) * 128], qtok[:, blk, hh * 128:(hh + 1) * 128],
                         ident[:, :], R=[qtok_t[blk]])
                b.op("act", lambda h: h.copy(qT[:, :], ptr[0][:, :]), R=[ptr_t[0]], W=[qT_t])
                b.op("dve", lambda h: h.tensor_tensor(qdT[:, :], ptr[0][:, :], DTAB, ALU.mult),
                     R=[ptr_t[0]], W=[qdT_t])
                for hh in range(4):
                    b.tr(ptr_t[1], ptr[1][:, hh * 128:(hh + 1) * 128], kb[:, hh * 128:(hh + 1) * 128],
                         ident[:, :], R=[kvb_t])
                b.op("act", lambda h: h.copy(kT[:, :], ptr[1][:, :]), R=[ptr_t[1]], W=[kT_t])
                p, p_t = next_pa()
                for hh in range(4):
                    c = slice(hh * 128, (hh + 1) * 128)
                    b.mm(p_t, p[:, c], [(kT[:, c], qT[:, c])], R=[kT_t, qT_t])
                b.op("dve", lambda h: h.tensor_tensor(PTm[:, :], p[:, :], MT, ALU.mult), R=[p_t], W=[PTm_t])
                b.op("dve", lambda h: h.tensor_tensor(
                    vw2[:, :].rearrange("p (h e) -> p h e", h=4), vb.rearrange("p (h e) -> p h e", h=4),
                    _bc(W2TAB, [128, 4, 128], (2,)), ALU.mult), R=[kvb_t], W=[vw2_t])
                for hh in range(4):
                    c = slice(hh * 128, (hh + 1) * 128)
                    b.mm(po_t, po[:, c], [(vb[:, c], PTm[:, c]), (Sb_[:, c], qdT[:, c])],
                         R=[kvb_t, PTm_t, Sb_t, qdT_t])
                for hh in range(4):
                    c = slice(hh * 128, (hh + 1) * 128)
                    b.mm(pS_t, pS[:, c], [(kb[:, c], vw2[:, c])], R=[kvb_t, vw2_t])
                for hh in range(4):
                    c = slice(hh * 128, (hh + 1) * 128)
                    b.op("dve", lambda h: h.scalar_tensor_tensor(
                        S[:, c], S[:, c], float(gam[hh] ** 128), pS[:, c], ALU.mult, ALU.add),
                        R=[pS_t], W=[S_t])
                b.op("act", lambda h: h.copy(Sb_[:, :], S[:, :]), R=[S_t], W=[Sb_t])
                b.op("act", lambda h: h.activation(osq[:, :], po[:, :], AF.Square), R=[po_t], W=[osq_t])
                b.mm(pst_t, pst[:, :], [(ones_h[:, :], osq[:, :])], R=[osq_t])
                rsqrt(rstdo[:, :], rstdo_t, pst[:, :], pst_t)
                b.op("dve", lambda h: h.tensor_tensor(bt[:, :], po[:, :], rstdo[:, :], ALU.mult),
                     R=[po_t, rstdo_t], W=[bt_t])
                b.op("dve", lambda h: h.tensor_tensor(
                    Bbuf[:, :, blk * 128:(blk + 1) * 128], bt[:, :].rearrange("p (h i) -> p h i", h=4),
                    sgn[:, :, blk * 128:(blk + 1) * 128], ALU.mult), R=[bt_t] + sgn_t, W=[Bb_t])
            step(None, do_ret)

    def emit_P2_C(l, t, Xap, X_tl, g2):
        for half in range(2):
            def do_out(s, st, half=half):
                Wv = s[:, :].rearrange("p (a n) -> p a n", a=8)
                for j in range(4):
                    oc = half * 4 + j
                    p, p_t = next_pa()
                    pairs = []
                    for kc in range(8):
                        rhs = Abuf[:, kc, :] if kc < 4 else Bbuf[:, kc - 4, :]
                        pairs.append((Wv[:, kc, j * 128:(j + 1) * 128], rhs))
                    b.mm(p_t, p[:, :], pairs, R=A_t + [Bb_t, st])
                    b.op("dve", lambda h: h.tensor_tensor(Xap[:, oc, :], Xap[:, oc, :], p[:, :], ALU.add),
                         R=[p_t], W=[X_tl])
            step([(0, 4096, w_out[l][:, :, half * 512:(half + 1) * 512])], do_out)

        step(None, lambda s, st: norm_tile(Xap, X_tl, g2, xnF, xnF_t))
        for grp in range(2):
            for pc in range(4):
                def do_f1(s, st, grp=grp, pc=pc):
                    Wv = s[:, :].rearrange("p (a n) -> p a n", a=8)
                    for j in range(4):
                        hc = pc * 4 + j
                        p, p_t = next_pa()
                        b.mm(p_t, p[:, :], [(Wv[:, kc, j * 128:(j + 1) * 128], xnF[:, kc, :]) for kc in range(8)],
                             R=[xnF_t, st])
                        i = hc % 2
                        b.op("act", lambda h: h.activation(relu_s[i][:, :], p[:, :], AF.Relu), R=[p_t], W=[relu_t[i]])
                        b.op("act", lambda h: h.activation(H[:, hc, :], relu_s[i][:, :], AF.Square),
                             R=[relu_t[i]], W=[H_t[hc]])
                c0 = (grp * 4 + pc) * 512
                step([(0, 4096, w_ff1[l][:, :, c0:c0 + 512])], do_f1)
            for pc in range(4):
                def do_f2(s, st, grp=grp, pc=pc):
                    Wv = s[:, :].rearrange("p (a n) -> p a n", a=16)
                    for j in range(2):
                        oc = pc * 2 + j
                        p, p_t = next_pa()
                        b.mm(p_t, p[:, :], [(Wv[:, k2, j * 128:(j + 1) * 128], H[:, k2, :]) for k2 in range(16)],
                             R=H_t + [st])
                        b.op("dve", lambda h: h.tensor_tensor(Xap[:, oc, :], Xap[:, oc, :], p[:, :], ALU.add),
                             R=[p_t], W=[X_tl])
                step([(0, 4096, w_ff2[l][:, grp * 16:(grp + 1) * 16, pc * 256:(pc + 1) * 256])], do_f2)

    for p, l in phases:
        if p == "P2":
            emit_P2_prologue(l)
    AB = [[] for _ in range(NT)]
    C = [[] for _ in range(NT)]
    Dl = [[] for _ in range(NT)]
    for t in range(NT):
        Xap = Xb[t % 2]
        X_tl = Xt_[t % 2]

        def ldx(s, st, t=t, Xap=Xap, X_tl=X_tl):
            b.dma("sp", Xap[:, :, :], xT_d[:, :, t * T:(t + 1) * T], W=[X_tl])
        cur[0] = AB[t]
        step(None, ldx)
        for p, l in phases:
            if p == "P2":
                cur[0] = AB[t]
                emit_P2(l, t, Xap, X_tl, "AB")
                cur[0] = C[t]
                emit_P2(l, t, Xap, X_tl, "C")
        cur[0] = Dl[t]
        for p, l in phases:
            if p == "P1":
                emit_P1(l, t, Xap, X_tl)
        if final:
            def fin_norm(s, st, t=t, Xap=Xap, X_tl=X_tl):
                for hf in range(2):
                    norm_tile(Xap, X_tl, 2 * PL, yout, yout_t, half=hf)
                    b.dma("sp", xT_out_d[:, hf * 4:(hf + 1) * 4, t * T:(t + 1) * T], yout[:, :, :], R=[yout_t])
            step(None, fin_norm)
        elif store_x:
            def stx(s, st, t=t, Xap=Xap, X_tl=X_tl):
                b.dma("sp", xT_out_d[:, :, t * T:(t + 1) * T], Xap[:, :, :], R=[X_tl])
            step(None, stx)
    cur[0] = steps
    steps.extend(AB[0])
    for t in range(NT):
        s1 = C[t]
        s2 = AB[t + 1] if t + 1 < NT else []
        for i in range(max(len(s1), len(s2))):
            if i < len(s1):
                steps.append(s1[i])
            if i < len(s2):
                steps.append(s2[i])
        steps.extend(Dl[t])
    run_steps()
    fin_deps = []
    for tl in Tl.all:
        fin_deps.append(tl.w)
        fin_deps.extend(tl.r)
    b._waits("sp", fin_deps)
    return nc


def _fm(a, kchunks):
    K, N = a.shape
    return np.ascontiguousarray(a.reshape(kchunks, 128, N).transpose(1, 0, 2))


def _consts(core):
    qt = core % 4
    gam = np.array([1.0 - 2.0 ** (-5 - h) for h in range(4)], np.float64)
    s = 128.0 ** -0.5
    inv_freq = (10000.0 ** (-np.arange(0, 128, 2, dtype=np.float32) / np.float32(128))).astype(np.float32)
    pos = (qt * TOK + np.arange(TOK, dtype=np.float32)).astype(np.float32)
    ang = (pos[:, None] * inv_freq[None, :]).astype(np.float32)
    c = np.cos(ang).astype(np.float32).reshape(NB, 128, 64).transpose(1, 0, 2)
    sn = np.sin(ang).astype(np.float32).reshape(NB, 128, 64).transpose(1, 0, 2)
    cm = np.zeros((128, CM_N), np.float32)
    i = np.arange(128)
    ii, jj = np.meshgrid(i, i, indexing="ij")
    for h in range(4):
        same = (ii // 64) == (jj // 64)
        causal = (ii // 64 == 1) & (jj // 64 == 0)
        M = np.where(same, gam[h] ** np.abs(ii - jj), np.where(causal, gam[h] ** np.maximum(ii - jj, 0), 0.0))
        cm[:, CM_MT + h * 128: CM_MT + (h + 1) * 128] = (s * M.T).astype(np.float32)
        cm[:, CM_DT + h * 128: CM_DT + (h + 1) * 128] = (gam[h] ** (i + 1.0))[None, :]
        cm[:, CM_W2 + h] = s * gam[h] ** (127.0 - i)
        for gb in range(NB):
            cm[:, CM_WT + gb * 4 + h] = s * gam[h] ** (2047.0 - (gb * 128 + i))
        for r in range(8):
            if r // 4 == core // 4 and r < core:
                cm[:, CM_COEF + r * 4 + h] = gam[h] ** (2048.0 * (core - 1 - r))
    cm[:, CM_ID:CM_ID + 128] = np.eye(128, dtype=np.float32)
    if qt > 0:
        cm[:, CM_SEL + core - 1] = 1.0
    return np.ascontiguousarray(c), np.ascontiguousarray(sn), cm


def _params(norm1_g, conv_w, conv_b, conv_ln_g, conv_ln_b, ret_norm_g, norm2_g, final_g):
    P = np.zeros((128, NPAR), np.float32)
    for l in range(NL):
        o = l * PL
        P[:, o:o + 8] = norm1_g[l].reshape(8, 128).T
        P[:, o + 8:o + 16] = norm2_g[l].reshape(8, 128).T
        cw = conv_w[l].reshape(CK, 4, 128)
        P[:, o + 16:o + 16 + 4 * CK] = cw.transpose(2, 1, 0).reshape(128, 4 * CK)
        P[:, o + 140:o + 144] = conv_b[l].reshape(4, 128).T
        P[:, o + 144:o + 148] = conv_ln_g[l].reshape(4, 128).T
        P[:, o + 148:o + 152] = conv_ln_b[l].reshape(4, 128).T
        P[:, o + 152:o + 156] = ret_norm_g[l].reshape(4, 128).T
    P[:, 2 * PL:2 * PL + 8] = final_g.reshape(8, 128).T
    return P


_NC_CACHE = {}


def _get_nc(key, *a, **k):
    if key not in _NC_CACHE:
        _NC_CACHE[key] = build(*a, **k)
    return _NC_CACHE[key]


def kernel(x, norm1_g, w_in, conv_w, conv_b, conv_ln_g, conv_ln_b, ret_norm_g,
           w_out, norm2_g, w_ff1, w_ff2, final_g):
    f = lambda a: np.asarray(a, dtype=np.float32)
    x, norm1_g, w_in, conv_w, conv_b = f(x), f(norm1_g), f(w_in), f(conv_w), f(conv_b)
    conv_ln_g, conv_ln_b, ret_norm_g, w_out = f(conv_ln_g), f(conv_ln_b), f(ret_norm_g), f(w_out)
    norm2_g, w_ff1, w_ff2, final_g = f(norm2_g), f(w_ff1), f(w_ff2), f(final_g)
    ncores = 8
    cores = list(range(ncores))
    P = _params(norm1_g, conv_w, conv_b, conv_ln_g, conv_ln_b, ret_norm_g, norm2_g, final_g)
    Win = [_fm(w_in[l], 8) for l in range(NL)]
    Wout = [_fm(w_out[l], 8) for l in range(NL)]
    Wf1 = [_fm(w_ff1[l], 8) for l in range(NL)]
    Wf2 = [_fm(w_ff2[l], 32) for l in range(NL)]
    cst = [_consts(c) for c in cores]
    xT = []
    for c in cores:
        bi, qt = c // 4, c % 4
        xs = x[bi, qt * TOK:(qt + 1) * TOK, :]
        xT.append(_fm(np.ascontiguousarray(xs.T), 8))

    def base(c):
        return {"params": P, "ctab": cst[c][0], "stab": cst[c][1], "cmisc": cst[c][2]}

    nc1 = _get_nc("L1", [("P1", 0)])
    maps = [dict(base(c), w_in0=Win[0], xT=xT[c]) for c in cores]
    r1 = run_bass_kernel_spmd(nc1, maps, core_ids=cores).results
    xch = np.ascontiguousarray(np.stack([r1[c]["xch_out"] for c in cores], 0).transpose(1, 0, 2))
    nc2 = _get_nc("L2", [("P2", 0), ("P1", 1)], store_x=True)
    maps = [dict(base(c), w_in0=Win[0], w_in1=Win[1], w_out0=Wout[0], w_ff10=Wf1[0], w_ff20=Wf2[0],
                 xT=xT[c], xch_in=xch, kv_in=r1[c]["kv_out"]) for c in cores]
    r2 = run_bass_kernel_spmd(nc2, maps, core_ids=cores).results
    xch = np.ascontiguousarray(np.stack([r2[c]["xch_out"] for c in cores], 0).transpose(1, 0, 2))
    nc3 = _get_nc("L3", [("P2", 1)], final=True)
    maps = [dict(base(c), w_in1=Win[1], w_out1=Wout[1], w_ff11=Wf1[1], w_ff21=Wf2[1],
                 xT=r2[c]["xT_out"], xch_in=xch, kv_in=r2[c]["kv_out"]) for c in cores]
    r3 = run_bass_kernel_spmd(nc3, maps, core_ids=cores).results
    out = np.empty((2, SEQ, D), np.float32)
    for c in cores:
        bi, qt = c // 4, c % 4
        y = r3[c]["xT_out"]
        out[bi, qt * TOK:(qt + 1) * TOK, :] = y.transpose(1, 0, 2).reshape(D, TOK).T
    return out
```

```python
import numpy as np
import ml_dtypes
import concourse.bass as bass
import concourse.mybir as mybir
from concourse.bass_utils import run_bass_kernel_spmd

F32 = mybir.dt.float32
BF16 = mybir.dt.bfloat16
ALU = mybir.AluOpType
AF = mybir.ActivationFunctionType

NL = 2
D = 1024
SEQ = 8192
TOK = 2048
T = 512
NT = TOK // T
NB = TOK // 128
EPS = 1e-6
CK = 31
PL = 156
NPAR = 2 * PL + 8
NSLOT = 3
DBG = 99
CM_MT = 0
CM_DT = 512
CM_W2 = 1024
CM_WT = 1028
CM_ID = 1092
CM_COEF = 1220
CM_SEL = 1252
CM_N = 1260
XW = 640


class Tl:
    all = []

    def __init__(self, psum=False):
        Tl.all.append(self)
        self.psum = psum
        self.w = None
        self.r = []
        self.sem = None
        self.cnt = 0


class B:
    def __init__(self, nc):
        self.nc = nc
        self.eng = {}
        for name, h in (("pe", nc.tensor), ("dve", nc.vector), ("act", nc.scalar),
                        ("pool", nc.gpsimd), ("sp", nc.sync)):
            sem = nc.alloc_semaphore("s_" + name)
            self.eng[name] = dict(h=h, sem=sem, cnt=0, seen={})
        self.nsem = 0

    def _waits(self, e, deps):
        E = self.eng[e]
        best = {}
        for d in deps:
            if d is None:
                continue
            sem, val = d
            if e == "pe" and sem is self.eng["pe"]["sem"]:
                continue
            k = id(sem)
            if k not in best or best[k][1] < val:
                best[k] = (sem, val)
        for k, (sem, val) in best.items():
            if E["seen"].get(k, 0) >= val:
                continue
            E["h"].wait_ge(sem, val)
            E["seen"][k] = val

    def _deps(self, R, W, deps, e=None):
        d = list(deps)
        for t in R:
            d.append(t.w)
            if t.psum:
                own = self.eng[e]["sem"]
                d.extend(tok for tok in t.r if tok[0] is not own)
        for t in W:
            d.append(t.w)
            d.extend(t.r)
        return d

    def _commit(self, tok, R, W):
        for t in R:
            t.r.append(tok)
        for t in W:
            t.w = tok
            t.r = []

    def op(self, e, fn, R=(), W=(), deps=()):
        E = self.eng[e]
        self._waits(e, self._deps(R, W, deps, e))
        ins = fn(E["h"])
        ins.then_inc(E["sem"], 1)
        E["cnt"] += 1
        tok = (E["sem"], E["cnt"])
        self._commit(tok, R, W)
        return tok

    def mm(self, Wt, out, pairs, R=(), start=True, stop=True, skip=False, deps=()):
        E = self.eng["pe"]
        self._waits("pe", self._deps(R, [Wt], deps, "pe"))
        n = len(pairs)
        ins = None
        for i, (l, r) in enumerate(pairs):
            kw = {}
            if skip:
                kw["skip_group_check"] = True
            ins = self.nc.tensor.matmul(out, l, r, start=(start and i == 0),
                                        stop=(stop and i == n - 1), **kw)
        ins.then_inc(E["sem"], 1)
        E["cnt"] += 1
        tok = (E["sem"], E["cnt"])
        self._commit(tok, R, [Wt])
        return tok

    def tr(self, Wt, out, in_, ident, R=()):
        E = self.eng["pe"]
        self._waits("pe", self._deps(R, [Wt], (), "pe"))
        ins = self.nc.tensor.transpose(out, in_, ident)
        ins.then_inc(E["sem"], 1)
        E["cnt"] += 1
        tok = (E["sem"], E["cnt"])
        self._commit(tok, R, [Wt])
        return tok

    def dma_multi(self, e, lst, t):
        E = self.eng[e]
        self._waits(e, self._deps((), [t], (), e))
        if t.sem is None:
            self.nsem += 1
            t.sem = self.nc.alloc_semaphore(f"d{self.nsem}")
        for out, in_ in lst:
            E["h"].dma_start(out=out, in_=in_).then_inc(t.sem, 16)
            t.cnt += 16
        tok = (t.sem, t.cnt)
        self._commit(tok, (), [t])
        return tok

    def dma(self, e, out, in_, R=(), W=(), deps=()):
        E = self.eng[e]
        self._waits(e, self._deps(R, W, deps, e))
        t = (list(W) + list(R))[0]
        if t.sem is None:
            self.nsem += 1
            t.sem = self.nc.alloc_semaphore(f"d{self.nsem}")
        E["h"].dma_start(out=out, in_=in_).then_inc(t.sem, 16)
        t.cnt += 16
        tok = (t.sem, t.cnt)
        self._commit(tok, R, W)
        return tok


def _bc(ap, shape, axes):
    for a in axes:
        ap = ap.unsqueeze(a)
    return ap.broadcast_to(list(shape))


def build(phases, store_x=False, final=False):
    nc = bass.Bass("TRN2", target_bir_lowering=False)
    Tl.all = []
    b = B(nc)
    lay = sorted({l for _, l in phases})
    has_p1 = [l for p, l in phases if p == "P1"]
    has_p2 = [l for p, l in phases if p == "P2"]

    def din(name, shape, dt=F32):
        return nc.dram_tensor(name, list(shape), dt, kind="ExternalInput").ap()

    def dout(name, shape, dt=F32):
        return nc.dram_tensor(name, list(shape), dt, kind="ExternalOutput").ap()

    w_in = {l: din(f"w_in{l}", [128, 8, 3072]) for l in lay}
    w_out = {l: din(f"w_out{l}", [128, 8, 1024]) for l in has_p2}
    w_ff1 = {l: din(f"w_ff1{l}", [128, 8, 4096]) for l in has_p2}
    w_ff2 = {l: din(f"w_ff2{l}", [128, 32, 1024]) for l in has_p2}
    params_d = din("params", [128, NPAR])
    ctab_d = din("ctab", [128, NB, 64])
    stab_d = din("stab", [128, NB, 64])
    cmisc_d = din("cmisc", [128, CM_N])
    xT_d = din("xT", [128, 8, TOK])
    if has_p2:
        xch_in_d = din("xch_in", [128, 8, XW])
        kv_in_d = din("kv_in", [128, NB, 2, 512], BF16)
    if has_p1:
        xch_out_d = dout("xch_out", [128, XW])
        kv_out_d = dout("kv_out", [128, NB, 2, 512], BF16)
    if store_x or final:
        xT_out_d = dout("xT_out", [128, 8, TOK])

    def sb(name, shape, dt=F32):
        return nc.alloc_sbuf_tensor(name, list(shape), dt)

    def ps(name, shape, dt=F32):
        return nc.alloc_psum_tensor(name, list(shape), dt)

    params = sb("params_sb", [128, NPAR]); params_t = Tl()
    ctab = sb("ctab_sb", [128, NB, 64]); ctab_t = Tl()
    stab = sb("stab_sb", [128, NB, 64]); stab_t = Tl()
    cmisc = sb("cmisc_sb", [128, CM_N]); cmisc_t = Tl()
    ident = sb("ident_sb", [128, 128], BF16); ident_t = Tl()
    ones_d = sb("ones_d", [128, 128], BF16)
    ones_c = sb("ones_c", [128, 128], BF16)
    ones_h = sb("ones_h", [128, 128], BF16)
    c_toks = [
        b.dma("sp", params[:, :], params_d, W=[params_t]),
        b.dma("sp", ctab[:, :, :], ctab_d, W=[ctab_t]),
        b.dma("sp", stab[:, :, :], stab_d, W=[stab_t]),
        b.dma("sp", cmisc[:, :], cmisc_d, W=[cmisc_t]),
        b.dma("pool", ident[:, :], cmisc_d[:, CM_ID:CM_ID + 128], W=[ident_t]),
    ]
    c_toks.append(b.op("dve", lambda h: h.memset(ones_d[:, :], 1.0 / 1024)))
    c_toks.append(b.op("dve", lambda h: h.memset(ones_c[:, :], 1.0 / 512)))
    c_toks.append(b.op("dve", lambda h: h.memset(ones_h[:, :], 1.0 / 128)))
    for e in ("pe", "dve", "act"):
        b._waits(e, c_toks)
    MT = cmisc[:, CM_MT:CM_MT + 512]
    DTAB = cmisc[:, CM_DT:CM_DT + 512]
    W2TAB = cmisc[:, CM_W2:CM_W2 + 4]

    def pcol(l, off):
        return params[:, l * PL + off: l * PL + off + 1]

    Xb = [sb(f"X{i}", [128, 8, T]) for i in range(2)]
    Xt_ = [Tl() for _ in range(2)]
    xn = sb("xn", [128, 8, T], BF16); xn_t = Tl()
    xnF = sb("xnF", [128, 8, T], BF16); xnF_t = Tl()
    sq = [sb(f"sq{i}", [128, T], BF16) for i in range(4)]; sq_t = [Tl() for _ in range(4)]
    rstd = sb("rstd", [128, T]); rstd_t = Tl()
    rt1 = sb("rt1", [128, 512]); rt1_t = Tl()
    rt2 = sb("rt2", [128, 512]); rt2_t = Tl()
    slots = [sb(f"wslot{i}", [128, 4096], BF16) for i in range(NSLOT)]
    slot_t = [Tl() for _ in range(NSLOT)]
    npool = 8 - 3 - (1 if has_p1 else 0)
    pa = [ps(f"pa{i}", [128, 512]) for i in range(npool)]; pa_t = [Tl(True) for _ in range(npool)]
    if has_p1:
        pS1 = ps("pS1", [128, 512]); pS1_t = Tl(True)
    pS = ps("pS", [128, 512]); pS_t = Tl(True)
    po = ps("po", [128, 512]); po_t = Tl(True)
    ptr_all = ps("ptr", [128, 1024], BF16)
    ptr = [ptr_all[:, 0:512], ptr_all[:, 512:1024]]; ptr_t = [Tl(True)] * 2
    pa_rr = [0]

    def next_pa():
        i = pa_rr[0] % npool
        pa_rr[0] += 1
        return pa[i], pa_t[i]

    if has_p1:
        ktok = sb("ktok", [128, 4, 512], BF16); ktok_t = [Tl() for _ in range(4)]
        vtok = [sb(f"vtok{i}", [128, 512], BF16) for i in range(2)]; vtok_t = [Tl(), Tl()]
        vW = [sb(f"vW{i}", [128, 512], BF16) for i in range(2)]; vW_t = [Tl(), Tl()]
        xch_sb = sb("xch_sb", [128, XW]); xch_sb_t = Tl()
        sgt = sb("sgt", [128, 128]); sgt_t = Tl()
    if has_p2:
        xr = [sb(f"xr{i}", [128, XW]) for i in range(2)]; xr_t = [Tl(), Tl()]
        hg = sb("hg", [128, 4, 30 + T], BF16); hg_t = [Tl() for _ in range(4)]; halo_t = Tl()
        dg = sb("dg", [128, CK, 128], BF16); dg_t = Tl()
        sig = [sb(f"sig{i}", [128, T]) for i in range(2)]; sig_t = [Tl(), Tl()]
        acc = sb("acc", [128, 4, T]); acc_t = [Tl() for _ in range(4)]
        abf = [sb(f"abf{i}", [128, T], BF16) for i in range(2)]; abf_t = [Tl(), Tl()]
        asq = [sb(f"asq{i}", [128, T], BF16) for i in range(2)]; asq_t = [Tl(), Tl()]
        mean = sb("mean", [128, T]); mean_t = Tl()
        rstdc = sb("rstdc", [128, T]); rstdc_t = Tl()
        lnt = [sb(f"lnt{i}", [128, T]) for i in range(2)]; lnt_t = [Tl(), Tl()]
        Abuf = sb("Abuf", [128, 4, T], BF16); A_t = [Tl() for _ in range(4)]
        Bbuf = sb("Bbuf", [128, 4, T], BF16); Bb_t = Tl()
        qtok = sb("qtok", [128, 4, 512], BF16); qtok_t = [Tl() for _ in range(4)]
        sgn = sb("sgn", [128, 4, T]); sgn_t = [Tl() for _ in range(4)]
        kvt = [sb("kvt0", [128, 4, 2, 512], BF16)] * 2; kvt_t = [Tl()] * 2
        qT = sb("qT", [128, 512], BF16); qT_t = Tl()
        qdT = sb("qdT", [128, 512], BF16); qdT_t = Tl()
        kT = sb("kT", [128, 512], BF16); kT_t = Tl()
        PTm = sb("PTm", [128, 512], BF16); PTm_t = Tl()
        vw2 = sb("vw2", [128, 512], BF16); vw2_t = Tl()
        S = sb("S", [128, 512]); S_t = Tl()
        Sb_ = sb("Sb", [128, 512], BF16); Sb_t = Tl()
        osq = sb("osq", [128, 512], BF16); osq_t = Tl()
        rstdo = sb("rstdo", [128, 512]); rstdo_t = Tl()
        bt = sb("bt", [128, 512]); bt_t = Tl()
        H = sb("H", [128, 8, T], BF16); H_t = [Tl() for _ in range(8)]
        relu_s = lnt; relu_t = lnt_t
    if final:
        yout = sb("yout", [128, 4, T]); yout_t = Tl()

    steps = []
    cur = [steps]

    def step(piece, fn):
        cur[0].append((piece, fn))

    def run_steps():
        pidx = [i for i, s in enumerate(steps) if s[0] is not None]
        issued = 0

        def issue(j):
            si = j % NSLOT
            lst = []
            for (c0, c1, src) in steps[pidx[j]][0]:
                dst = slots[si][:, c0:c1]
                if len(src.shape) == 3:
                    dst = dst.rearrange("p (a n) -> p a n", a=src.shape[1])
                lst.append((dst, src))
            b.dma_multi("pool", lst, slot_t[si])

        k = 0
        for i, (piece, fn) in enumerate(steps):
            if piece is not None:
                while issued < len(pidx) and issued <= k + NSLOT - 1:
                    issue(issued)
                    issued += 1
                si = k % NSLOT
                fn(slots[si], slot_t[si])
                k += 1
            else:
                fn(None, None)

    rq = sb("rq", [128, 512]); rq_t = Tl()

    def rsqrt(dst, dst_t, src, src_t):
        b.op("act", lambda h: h.activation(rq[:, :], src, AF.Sqrt, bias=EPS), R=[src_t], W=[rq_t])
        b.op("dve", lambda h: h.reciprocal(dst, rq[:, :]), R=[rq_t], W=[dst_t])

    def norm_stats_a(Xap, X_tl):
        for kc in range(4):
            b.op("act", lambda h: h.activation(sq[kc][:, :], Xap[:, kc, :], AF.Square),
                 R=[X_tl], W=[sq_t[kc]])

    def norm_stats_b(Xap, X_tl, bank):
        p, p_t = bank
        for kc in range(4):
            b.mm(p_t, p[:, :], [(ones_d[:, :], sq[kc][:, :])], R=[sq_t[kc]], start=(kc == 0), stop=False)
        for kc in range(4, 8):
            b.op("act", lambda h: h.activation(sq[kc - 4][:, :], Xap[:, kc, :], AF.Square),
                 R=[X_tl], W=[sq_t[kc - 4]])

    def norm_stats_c(bank):
        p, p_t = bank
        for kc in range(4, 8):
            b.mm(p_t, p[:, :], [(ones_d[:, :], sq[kc - 4][:, :])], R=[sq_t[kc - 4]], start=False, stop=(kc == 7))
        rsqrt(rstd[:, :], rstd_t, p[:, :], p_t)

    def norm_apply(Xap, X_tl, goff, out3, out_tl, rng=range(8), ko=0):
        for kc in rng:
            b.op("dve", lambda h: h.scalar_tensor_tensor(
                out3[:, kc - ko, :], Xap[:, kc, :], params[:, goff + kc: goff + kc + 1], rstd[:, :],
                ALU.mult, ALU.mult), R=[X_tl, rstd_t], W=[out_tl])

    def norm_steps(Xap, X_tl, goff, out3, out_tl):
        def f(s, st):
            bank = next_pa()
            norm_stats_a(Xap, X_tl)
            norm_stats_b(Xap, X_tl, bank)
            norm_stats_c(bank)
            norm_apply(Xap, X_tl, goff, out3, out_tl)
        step(None, f)

    def rotary(psrc, psrc_t, blk_g, dst, dst_t):
        x4 = psrc.rearrange("p (h two d) -> p h two d", h=4, two=2)
        c = ctab[:, blk_g, :]
        s = stab[:, blk_g, :]
        t1v = rt1[:, :].rearrange("p (h two d) -> p h two d", h=4, two=2)
        t2v = rt2[:, :].rearrange("p (h two d) -> p h two d", h=4, two=2)
        b.op("dve", lambda h: h.tensor_tensor(t1v, x4, _bc(c, [128, 4, 2, 64], (1, 1)), ALU.mult),
             R=[psrc_t], W=[rt1_t])
        b.op("dve", lambda h: h.scalar_tensor_tensor(
            t2v[:, :, 0, :], x4[:, :, 1, :], -1.0, _bc(s, [128, 4, 64], (1,)), ALU.mult, ALU.mult),
            R=[psrc_t], W=[rt2_t])
        b.op("dve", lambda h: h.tensor_tensor(
            t2v[:, :, 1, :], x4[:, :, 0, :], _bc(s, [128, 4, 64], (1,)), ALU.mult),
            R=[psrc_t], W=[rt2_t])
        b.op("dve", lambda h: h.tensor_tensor(dst, rt1[:, :], rt2[:, :], ALU.add),
             R=[rt1_t, rt2_t], W=[dst_t])

    def win_piece(l, c0, n=512):
        return [(0, 8 * n, w_in[l][:, :, c0:c0 + n])]

    def glu_piece(l, hp):
        return [(0, 2048, w_in[l][:, :, hp * 256:hp * 256 + 256]),
                (2048, 4096, w_in[l][:, :, 512 + hp * 256:512 + hp * 256 + 256])]

    def emit_P1(l, t, Xap, X_tl):
        g1 = l * PL + 0
        norm_steps(Xap, X_tl, g1, xn, xn_t)

        def do_k(s, st):
            Wv = s[:, :].rearrange("p (a n) -> p a n", a=8)
            for blk in range(4):
                p, p_t = next_pa()
                b.mm(p_t, p[:, :], [(xn[:, kc, blk * 128:(blk + 1) * 128], Wv[:, kc, :]) for kc in range(8)],
                     R=[xn_t, st])
                rotary(p[:, :], p_t, t * 4 + blk, ktok[:, blk, :], ktok_t[blk])
                b.dma("sp", kv_out_d[:, t * 4 + blk, 0, :], ktok[:, blk, :], R=[ktok_t[blk]])
        step(win_piece(l, 1536), do_k)

        def do_v(s, st):
            Wv = s[:, :].rearrange("p (a n) -> p a n", a=8)
            for blk in range(4):
                gb = t * 4 + blk
                i = blk % 2
                p, p_t = next_pa()
                b.mm(p_t, p[:, :], [(xn[:, kc, blk * 128:(blk + 1) * 128], Wv[:, kc, :]) for kc in range(8)],
                     R=[xn_t, st])
                b.op("act", lambda h: h.copy(vtok[i][:, :], p[:, :]), R=[p_t], W=[vtok_t[i]])
                b.dma("sp", kv_out_d[:, gb, 1, :], vtok[i][:, :], R=[vtok_t[i]])
                wt = cmisc[:, CM_WT + gb * 4: CM_WT + gb * 4 + 4]
                b.op("dve", lambda h: h.tensor_tensor(
                    vW[i][:, :].rearrange("p (h e) -> p h e", h=4), p[:, :].rearrange("p (h e) -> p h e", h=4),
                    _bc(wt, [128, 4, 128], (2,)), ALU.mult), R=[p_t], W=[vW_t[i]])
                for hh in range(4):
                    b.mm(pS1_t, pS1[:, hh * 128:(hh + 1) * 128],
                         [(ktok[:, blk, hh * 128:(hh + 1) * 128], vW[i][:, hh * 128:(hh + 1) * 128])],
                         R=[ktok_t[blk], vW_t[i]], start=(gb == 0 and hh == 0), stop=(gb == NB - 1 and hh == 3),
                         skip=True)
        step(win_piece(l, 2048), do_v)

        if t == NT - 1:
            banks = []
            for hp in range(2):
                def do_tail_piece(s, st, hp=hp):
                    if hp == 0:
                        banks.append(next_pa())
                        banks.append(next_pa())
                    (pA, pA_t), (pG, pG_t) = banks
                    Wa = s[:, 0:2048].rearrange("p (a n) -> p a n", a=8)
                    Wg = s[:, 2048:4096].rearrange("p (a n) -> p a n", a=8)
                    for j in range(2):
                        cc = hp * 2 + j
                        b.mm(pA_t, pA[:, cc * 32:(cc + 1) * 32],
                             [(Wa[:, kc, j * 128:(j + 1) * 128], xn[:, kc, T - 32:T]) for kc in range(8)],
                             R=[xn_t, st])
                        b.mm(pG_t, pG[:, cc * 32:(cc + 1) * 32],
                             [(Wg[:, kc, j * 128:(j + 1) * 128], xn[:, kc, T - 32:T]) for kc in range(8)],
                             R=[xn_t, st])
                    if hp == 1:
                        b.op("act", lambda h: h.activation(sgt[:, :], pG[:, 0:128], AF.Sigmoid),
                             R=[pG_t], W=[sgt_t])
                        b.op("dve", lambda h: h.tensor_tensor(xch_sb[:, 512:640], pA[:, 0:128], sgt[:, :], ALU.mult),
                             R=[pA_t, sgt_t], W=[xch_sb_t])
                step(glu_piece(l, hp), do_tail_piece)

            def fin(s, st):
                b.op("act", lambda h: h.copy(xch_sb[:, 0:512], pS1[:, :]), R=[pS1_t], W=[xch_sb_t])
                b.dma("sp", xch_out_d, xch_sb[:, :], R=[xch_sb_t])
            step(None, fin)

    def emit_P2_prologue(l):
        def pro(s, st):
            for r in range(8):
                x_ = xr[r % 2]
                x_t = xr_t[r % 2]
                b.dma("sp", x_[:, :], xch_in_d[:, r, :], W=[x_t])
                for hh in range(4):
                    cf = cmisc[:, CM_COEF + r * 4 + hh: CM_COEF + r * 4 + hh + 1]
                    src = x_[:, hh * 128:(hh + 1) * 128]
                    dst = S[:, hh * 128:(hh + 1) * 128]
                    if r == 0:
                        b.op("dve", lambda h: h.tensor_scalar(dst, src, cf, None, ALU.mult),
                             R=[x_t], W=[S_t])
                    else:
                        b.op("dve", lambda h: h.scalar_tensor_tensor(dst, src, cf, dst, ALU.mult, ALU.add),
                             R=[x_t], W=[S_t])
                sl = cmisc[:, CM_SEL + r: CM_SEL + r + 1]
                src = x_[:, 512:640].rearrange("p (c n) -> p c n", c=4)[:, :, 2:32]
                dst = hg[:, :, 0:30]
                if r == 0:
                    b.op("dve", lambda h: h.tensor_scalar(dst, src, sl, None, ALU.mult),
                         R=[x_t], W=[halo_t])
                else:
                    b.op("dve", lambda h: h.scalar_tensor_tensor(dst, src, sl, dst, ALU.mult, ALU.add),
                         R=[x_t], W=[halo_t])
            b.op("act", lambda h: h.copy(Sb_[:, :], S[:, :]), R=[S_t], W=[Sb_t])
        step(None, pro)

    def emit_P2_AB(l, t, Xap, X_tl):
        g1 = l * PL + 0
        cw0 = l * PL + 16
        cb0 = l * PL + 140
        lg0 = l * PL + 144
        lb0 = l * PL + 148
        ng0 = l * PL + 152
        kvb = kvt[t % 2]
        kvb_t = kvt_t[t % 2]

        step(None, lambda s, st: b.dma("sp", kvb[:, :, :, :], kv_in_d[:, t * 4:(t + 1) * 4, :, :], W=[kvb_t]))
        norm_steps(Xap, X_tl, g1, xn, xn_t)

        for hp in range(2):
            def do_glu(s, st, hp=hp):
                Wa = s[:, 0:2048].rearrange("p (a n) -> p a n", a=8)
                Wg = s[:, 2048:4096].rearrange("p (a n) -> p a n", a=8)
                for j in range(2):
                    cc = hp * 2 + j
                    p1, p1_t = next_pa()
                    p2, p2_t = next_pa()
                    b.mm(p1_t, p1[:, :], [(Wa[:, kc, j * 128:(j + 1) * 128], xn[:, kc, :]) for kc in range(8)],
                         R=[xn_t, st])
                    b.mm(p2_t, p2[:, :], [(Wg[:, kc, j * 128:(j + 1) * 128], xn[:, kc, :]) for kc in range(8)],
                         R=[xn_t, st])
                    i = cc % 2
                    b.op("act", lambda h: h.activation(sig[i][:, :], p2[:, :], AF.Sigmoid), R=[p2_t], W=[sig_t[i]])
                    b.op("dve", lambda h: h.tensor_tensor(hg[:, cc, 30:30 + T], p1[:, :], sig[i][:, :], ALU.mult),
                         R=[p1_t, sig_t[i]], W=[hg_t[cc]])
            step(glu_piece(l, hp), do_glu)

        def do_q(s, st):
            Wv = s[:, :].rearrange("p (a n) -> p a n", a=8)
            for blk in range(4):
                p, p_t = next_pa()
                b.mm(p_t, p[:, :], [(xn[:, kc, blk * 128:(blk + 1) * 128], Wv[:, kc, :]) for kc in range(8)],
                     R=[xn_t, st])
                rotary(p[:, :], p_t, t * 4 + blk, qtok[:, blk, :], qtok_t[blk])
        step(win_piece(l, 1024), do_q)

        def do_g(s, st):
            Wv = s[:, :].rearrange("p (a n) -> p a n", a=8)
            for hh in range(4):
                p, p_t = next_pa()
                b.mm(p_t, p[:, :], [(Wv[:, kc, hh * 128:(hh + 1) * 128], xn[:, kc, :]) for kc in range(8)],
                     R=[xn_t, st])
                i = hh % 2
                b.op("act", lambda h: h.activation(sig[i][:, :], p[:, :], AF.Sigmoid), R=[p_t], W=[sig_t[i]])
                b.op("dve", lambda h: h.scalar_tensor_tensor(
                    sgn[:, hh, :], p[:, :], params[:, ng0 + hh: ng0 + hh + 1], sig[i][:, :], ALU.mult, ALU.mult),
                    R=[p_t, sig_t[i]], W=[sgn_t[hh]])
        step(win_piece(l, 2560), do_g)

        for cc in range(4):
            def conv_build(s, st, cc=cc):
                wv = params[:, cw0 + cc * CK: cw0 + (cc + 1) * CK]
                b.op("dve", lambda h: h.tensor_tensor(
                    dg[:, :, :], _bc(ident[:, :], [128, CK, 128], (1,)), _bc(wv, [128, CK, 128], (2,)), ALU.mult),
                    W=[dg_t])
            step(None, conv_build)

            def conv_mm(s, st, cc=cc):
                p, p_t = next_pa()
                b.mm(p_t, p[:, :], [(dg[:, j, :], hg[:, cc, j:j + T]) for j in range(CK)],
                     R=[dg_t, hg_t[cc], halo_t])
                b.op("act", lambda h: h.activation(acc[:, cc, :], p[:, :], AF.Identity,
                                                   bias=params[:, cb0 + cc: cb0 + cc + 1]),
                     R=[p_t], W=[acc_t[cc]])
            step(None, conv_mm)

        lnb = []

        def ln_a(s, st):
            b.op("dve", lambda h: h.tensor_copy(hg[:, :, 0:30], hg[:, :, T:T + 30]), R=hg_t, W=[halo_t])
            lnb.append(next_pa())
            lnb.append(next_pa())
            (pm, pm_t), (pq, pq_t) = lnb
            for cc in range(4):
                i = cc % 2
                b.op("act", lambda h: h.copy(abf[i][:, :], acc[:, cc, :]), R=[acc_t[cc]], W=[abf_t[i]])
                b.op("act", lambda h: h.activation(asq[i][:, :], acc[:, cc, :], AF.Square), R=[acc_t[cc]], W=[asq_t[i]])
                b.mm(pm_t, pm[:, :], [(ones_c[:, :], abf[i][:, :])], R=[abf_t[i]], start=(cc == 0), stop=(cc == 3))
                b.mm(pq_t, pq[:, :], [(ones_c[:, :], asq[i][:, :])], R=[asq_t[i]], start=(cc == 0), stop=(cc == 3))
            b.op("act", lambda h: h.copy(mean[:, :], pm[:, :]), R=[pm_t], W=[mean_t])
            b.op("dve", lambda h: h.tensor_tensor(lnt[0][:, :], mean[:, :], mean[:, :], ALU.mult),
                 R=[mean_t], W=[lnt_t[0]])
            b.op("dve", lambda h: h.tensor_tensor(lnt[1][:, :], pq[:, :], lnt[0][:, :], ALU.subtract),
                 R=[pq_t, lnt_t[0]], W=[lnt_t[1]])
            rsqrt(rstdc[:, :], rstdc_t, lnt[1][:, :], lnt_t[1])
        step(None, ln_a)

        for cc in range(4):
            def ln_c(s, st, cc=cc):
                b.op("dve", lambda h: h.tensor_tensor(lnt[0][:, :], acc[:, cc, :], mean[:, :], ALU.subtract),
                     R=[acc_t[cc], mean_t], W=[lnt_t[0]])
                b.op("dve", lambda h: h.tensor_tensor(lnt[1][:, :], lnt[0][:, :], rstdc[:, :], ALU.mult),
                     R=[lnt_t[0], rstdc_t], W=[lnt_t[1]])
                b.op("dve", lambda h: h.tensor_scalar(
                    lnt[0][:, :], lnt[1][:, :], params[:, lg0 + cc: lg0 + cc + 1],
                    params[:, lb0 + cc: lb0 + cc + 1], ALU.mult, ALU.add), R=[lnt_t[1]], W=[lnt_t[0]])
                i = cc % 2
                b.op("act", lambda h: h.activation(sig[i][:, :], lnt[0][:, :], AF.Sigmoid),
                     R=[lnt_t[0]], W=[sig_t[i]])
                b.op("dve", lambda h: h.tensor_tensor(Abuf[:, cc, :], lnt[0][:, :], sig[i][:, :], ALU.mult),
                     R=[lnt_t[0], sig_t[i]], W=[A_t[cc]])
            step(None, ln_c)

        gam = [1.0 - 2.0 ** (-5 - hh) for hh in range(4)]
        for blk in range(4):
            kb = kvb[:, blk, 0, :]
            vb = kvb[:, blk, 1, :]
            st8 = {}

            def ret_T(s, st, blk=blk, kb=kb, vb=vb):
                for hh in range(4):
                    b.tr(ptr_t[0], ptr[0][:, hh * 128:(hh + 1) * 128], qtok[:, blk, hh * 128:(hh + 1) * 128],
                         ident[:, :], R=[qtok_t[blk]])
                for hh in range(4):
                    b.tr(ptr_t[1], ptr[1][:, hh * 128:(hh + 1) * 128], kb[:, hh * 128:(hh + 1) * 128],
                         ident[:, :], R=[kvb_t])
                b.op("act", lambda h: h.copy(qT[:, :], ptr[0][:, :]), R=[ptr_t[0]], W=[qT_t])
                b.op("dve", lambda h: h.tensor_tensor(qdT[:, :], ptr[0][:, :], DTAB, ALU.mult),
                     R=[ptr_t[0]], W=[qdT_t])
                b.op("act", lambda h: h.copy(kT[:, :], ptr[1][:, :]), R=[ptr_t[1]], W=[kT_t])
                b.op("dve", lambda h: h.tensor_tensor(
                    vw2[:, :].rearrange("p (h e) -> p h e", h=4), vb.rearrange("p (h e) -> p h e", h=4),
                    _bc(W2TAB, [128, 4, 128], (2,)), ALU.mult), R=[kvb_t], W=[vw2_t])
            step(None, ret_T)

            def ret_P(s, st, blk=blk, kb=kb, vb=vb, st8=st8):
                p, p_t = next_pa()
                for hh in range(4):
                    c = slice(hh * 128, (hh + 1) * 128)
                    b.mm(p_t, p[:, c], [(kT[:, c], qT[:, c])], R=[kT_t, qT_t])
                b.op("dve", lambda h: h.tensor_tensor(PTm[:, :], p[:, :], MT, ALU.mult), R=[p_t], W=[PTm_t])
                for hh in range(4):
                    c = slice(hh * 128, (hh + 1) * 128)
                    b.mm(pS_t, pS[:, c], [(kb[:, c], vw2[:, c])], R=[kvb_t, vw2_t])
            step(None, ret_P)

            def ret_O(s, st, blk=blk, kb=kb, vb=vb, st8=st8):
                for hh in range(4):
                    c = slice(hh * 128, (hh + 1) * 128)
                    b.mm(po_t, po[:, c], [(vb[:, c], PTm[:, c]), (Sb_[:, c], qdT[:, c])],
                         R=[kvb_t, PTm_t, Sb_t, qdT_t])
                for hh in range(4):
                    c = slice(hh * 128, (hh + 1) * 128)
                    b.op("dve", lambda h: h.scalar_tensor_tensor(
                        S[:, c], S[:, c], float(gam[hh] ** 128), pS[:, c], ALU.mult, ALU.add),
                        R=[pS_t], W=[S_t])
                b.op("act", lambda h: h.copy(Sb_[:, :], S[:, :]), R=[S_t], W=[Sb_t])
                b.op("act", lambda h: h.activation(osq[:, :], po[:, :], AF.Square), R=[po_t], W=[osq_t])
            step(None, ret_O)

            def ret_N(s, st, blk=blk, st8=st8):
                pn, pn_t = next_pa()
                b.mm(pn_t, pn[:, :], [(ones_h[:, :], osq[:, :])], R=[osq_t])
                rsqrt(rstdo[:, :], rstdo_t, pn[:, :], pn_t)
                b.op("dve", lambda h: h.tensor_tensor(bt[:, :], po[:, :], rstdo[:, :], ALU.mult),
                     R=[po_t, rstdo_t], W=[bt_t])
                b.op("dve", lambda h: h.tensor_tensor(
                    Bbuf[:, :, blk * 128:(blk + 1) * 128], bt[:, :].rearrange("p (h i) -> p h i", h=4),
                    sgn[:, :, blk * 128:(blk + 1) * 128], ALU.mult), R=[bt_t] + sgn_t, W=[Bb_t])
            step(None, ret_N)

    def emit_P2_C(l, t, Xap, X_tl):
        g2 = l * PL + 8
        for half in range(2):
            def do_out(s, st, half=half):
                Wv = s[:, :].rearrange("p (a n) -> p a n", a=8)
                for j in range(4):
                    oc = half * 4 + j
                    p, p_t = next_pa()
                    pairs = []
                    for kc in range(8):
                        rhs = Abuf[:, kc, :] if kc < 4 else Bbuf[:, kc - 4, :]
                        pairs.append((Wv[:, kc, j * 128:(j + 1) * 128], rhs))
                    b.mm(p_t, p[:, :], pairs, R=A_t + [Bb_t, st])
                    b.op("dve", lambda h: h.tensor_tensor(Xap[:, oc, :], Xap[:, oc, :], p[:, :], ALU.add),
                         R=[p_t], W=[X_tl])
            step([(0, 4096, w_out[l][:, :, half * 512:(half + 1) * 512])], do_out)

        norm_steps(Xap, X_tl, g2, xnF, xnF_t)
        for grp in range(4):
            for pc in range(2):
                def do_f1(s, st, grp=grp, pc=pc):
                    Wv = s[:, :].rearrange("p (a n) -> p a n", a=8)
                    for j in range(4):
                        hc = pc * 4 + j
                        p, p_t = next_pa()
                        b.mm(p_t, p[:, :], [(Wv[:, kc, j * 128:(j + 1) * 128], xnF[:, kc, :]) for kc in range(8)],
                             R=[xnF_t, st])
                        i = hc % 2
                        b.op("act", lambda h: h.activation(relu_s[i][:, :], p[:, :], AF.Relu), R=[p_t], W=[relu_t[i]])
                        b.op("act", lambda h: h.activation(H[:, hc, :], relu_s[i][:, :], AF.Square),
                             R=[relu_t[i]], W=[H_t[hc]])
                c0 = (grp * 2 + pc) * 512
                step([(0, 4096, w_ff1[l][:, :, c0:c0 + 512])], do_f1)
            for pc in range(2):
                def do_f2(s, st, grp=grp, pc=pc):
                    Wv = s[:, :].rearrange("p (a n) -> p a n", a=8)
                    for j in range(4):
                        oc = pc * 4 + j
                        p, p_t = next_pa()
                        b.mm(p_t, p[:, :], [(Wv[:, k2, j * 128:(j + 1) * 128], H[:, k2, :]) for k2 in range(8)],
                             R=H_t + [st])
                        b.op("dve", lambda h: h.tensor_tensor(Xap[:, oc, :], Xap[:, oc, :], p[:, :], ALU.add),
                             R=[p_t], W=[X_tl])
                step([(0, 4096, w_ff2[l][:, grp * 8:(grp + 1) * 8, pc * 512:(pc + 1) * 512])], do_f2)

    for p, l in phases:
        if p == "P2":
            emit_P2_prologue(l)
    AB = [[] for _ in range(NT)]
    C = [[] for _ in range(NT)]
    Dl = [[] for _ in range(NT)]
    for t in range(NT):
        Xap = Xb[t % 2]
        X_tl = Xt_[t % 2]

        def ldx(s, st, t=t, Xap=Xap, X_tl=X_tl):
            b.dma("sp", Xap[:, :, :], xT_d[:, :, t * T:(t + 1) * T], W=[X_tl])
        cur[0] = AB[t]
        step(None, ldx)
        for p, l in phases:
            if p == "P2":
                cur[0] = AB[t]
                emit_P2_AB(l, t, Xap, X_tl)
                cur[0] = C[t]
                emit_P2_C(l, t, Xap, X_tl)
        cur[0] = Dl[t]
        for p, l in phases:
            if p == "P1":
                emit_P1(l, t, Xap, X_tl)
        if final:
            def fin_norm(s, st, t=t, Xap=Xap, X_tl=X_tl):
                bank = next_pa()
                norm_stats_a(Xap, X_tl)
                norm_stats_b(Xap, X_tl, bank)
                norm_stats_c(bank)
                for hf in range(2):
                    norm_apply(Xap, X_tl, 2 * PL, yout, yout_t, rng=range(hf * 4, hf * 4 + 4), ko=hf * 4)
                    b.dma("sp", xT_out_d[:, hf * 4:(hf + 1) * 4, t * T:(t + 1) * T], yout[:, :, :], R=[yout_t])
            step(None, fin_norm)
        elif store_x:
            def stx(s, st, t=t, Xap=Xap, X_tl=X_tl):
                b.dma("sp", xT_out_d[:, :, t * T:(t + 1) * T], Xap[:, :, :], R=[X_tl])
            step(None, stx)
    cur[0] = steps
    steps.extend(AB[0])
    for t in range(NT):
        s1 = C[t]
        s2 = AB[t + 1] if t + 1 < NT else []
        j2 = 0
        for i in range(len(s1)):
            steps.append(s1[i])
            tgt = ((i + 1) * len(s2) + len(s1) - 1) // len(s1)
            while j2 < min(tgt, len(s2)):
                steps.append(s2[j2])
                j2 += 1
        steps.extend(s2[j2:])
        steps.extend(Dl[t])
    run_steps()
    fin_deps = []
    for tl in Tl.all:
        fin_deps.append(tl.w)
        fin_deps.extend(tl.r)
    b._waits("sp", fin_deps)
    return nc


def _fm(a, kchunks):
    K, N = a.shape
    return np.ascontiguousarray(a.reshape(kchunks, 128, N).transpose(1, 0, 2))


def _consts(core):
    qt = core % 4
    gam = np.array([1.0 - 2.0 ** (-5 - h) for h in range(4)], np.float64)
    s = 128.0 ** -0.5
    inv_freq = (10000.0 ** (-np.arange(0, 128, 2, dtype=np.float32) / np.float32(128))).astype(np.float32)
    pos = (qt * TOK + np.arange(TOK, dtype=np.float32)).astype(np.float32)
    ang = (pos[:, None] * inv_freq[None, :]).astype(np.float32)
    c = np.cos(ang).astype(np.float32).reshape(NB, 128, 64).transpose(1, 0, 2)
    sn = np.sin(ang).astype(np.float32).reshape(NB, 128, 64).transpose(1, 0, 2)
    cm = np.zeros((128, CM_N), np.float32)
    i = np.arange(128)
    ii, jj = np.meshgrid(i, i, indexing="ij")
    for h in range(4):
        same = (ii // 64) == (jj // 64)
        causal = (ii // 64 == 1) & (jj // 64 == 0)
        M = np.where(same, gam[h] ** np.abs(ii - jj), np.where(causal, gam[h] ** np.maximum(ii - jj, 0), 0.0))
        cm[:, CM_MT + h * 128: CM_MT + (h + 1) * 128] = (s * M.T).astype(np.float32)
        cm[:, CM_DT + h * 128: CM_DT + (h + 1) * 128] = (gam[h] ** (i + 1.0))[None, :]
        cm[:, CM_W2 + h] = s * gam[h] ** (127.0 - i)
        for gb in range(NB):
            cm[:, CM_WT + gb * 4 + h] = s * gam[h] ** (2047.0 - (gb * 128 + i))
        for r in range(8):
            if r // 4 == core // 4 and r < core:
                cm[:, CM_COEF + r * 4 + h] = gam[h] ** (2048.0 * (core - 1 - r))
    cm[:, CM_ID:CM_ID + 128] = np.eye(128, dtype=np.float32)
    if qt > 0:
        cm[:, CM_SEL + core - 1] = 1.0
    return np.ascontiguousarray(c), np.ascontiguousarray(sn), cm


def _params(norm1_g, conv_w, conv_b, conv_ln_g, conv_ln_b, ret_norm_g, norm2_g, final_g):
    P = np.zeros((128, NPAR), np.float32)
    for l in range(NL):
        o = l * PL
        P[:, o:o + 8] = norm1_g[l].reshape(8, 128).T
        P[:, o + 8:o + 16] = norm2_g[l].reshape(8, 128).T
        cw = conv_w[l].reshape(CK, 4, 128)
        P[:, o + 16:o + 16 + 4 * CK] = cw.transpose(2, 1, 0).reshape(128, 4 * CK)
        P[:, o + 140:o + 144] = conv_b[l].reshape(4, 128).T
        P[:, o + 144:o + 148] = conv_ln_g[l].reshape(4, 128).T
        P[:, o + 148:o + 152] = conv_ln_b[l].reshape(4, 128).T
        P[:, o + 152:o + 156] = ret_norm_g[l].reshape(4, 128).T
    P[:, 2 * PL:2 * PL + 8] = final_g.reshape(8, 128).T
    return P


_NC_CACHE = {}


def _get_nc(key, *a, **k):
    if key not in _NC_CACHE:
        _NC_CACHE[key] = build(*a, **k)
    return _NC_CACHE[key]


def kernel(x, norm1_g, w_in, conv_w, conv_b, conv_ln_g, conv_ln_b, ret_norm_g,
           w_out, norm2_g, w_ff1, w_ff2, final_g):
    f = lambda a: np.asarray(a, dtype=np.float32)
    x, norm1_g, w_in, conv_w, conv_b = f(x), f(norm1_g), f(w_in), f(conv_w), f(conv_b)
    conv_ln_g, conv_ln_b, ret_norm_g, w_out = f(conv_ln_g), f(conv_ln_b), f(ret_norm_g), f(w_out)
    norm2_g, w_ff1, w_ff2, final_g = f(norm2_g), f(w_ff1), f(w_ff2), f(final_g)
    ncores = 8
    cores = list(range(ncores))
    P = _params(norm1_g, conv_w, conv_b, conv_ln_g, conv_ln_b, ret_norm_g, norm2_g, final_g)
    Win = [_fm(w_in[l], 8) for l in range(NL)]
    Wout = [_fm(w_out[l], 8) for l in range(NL)]
    Wf1 = [_fm(w_ff1[l], 8) for l in range(NL)]
    Wf2 = [_fm(w_ff2[l], 32) for l in range(NL)]
    cst = [_consts(c) for c in cores]
    xT = []
    for c in cores:
        bi, qt = c // 4, c % 4
        xs = x[bi, qt * TOK:(qt + 1) * TOK, :]
        xT.append(_fm(np.ascontiguousarray(xs.T), 8))

    def base(c):
        return {"params": P, "ctab": cst[c][0], "stab": cst[c][1], "cmisc": cst[c][2]}

    nc1 = _get_nc("L1", [("P1", 0)])
    maps = [dict(base(c), w_in0=Win[0], xT=xT[c]) for c in cores]
    r1 = run_bass_kernel_spmd(nc1, maps, core_ids=cores).results
    xch = np.ascontiguousarray(np.stack([r1[c]["xch_out"] for c in cores], 0).transpose(1, 0, 2))
    nc2 = _get_nc("L2", [("P2", 0), ("P1", 1)], store_x=True)
    maps = [dict(base(c), w_in0=Win[0], w_in1=Win[1], w_out0=Wout[0], w_ff10=Wf1[0], w_ff20=Wf2[0],
                 xT=xT[c], xch_in=xch, kv_in=r1[c]["kv_out"]) for c in cores]
    r2 = run_bass_kernel_spmd(nc2, maps, core_ids=cores).results
    xch = np.ascontiguousarray(np.stack([r2[c]["xch_out"] for c in cores], 0).transpose(1, 0, 2))
    nc3 = _get_nc("L3", [("P2", 1)], final=True)
    maps = [dict(base(c), w_in1=Win[1], w_out1=Wout[1], w_ff11=Wf1[1], w_ff21=Wf2[1],
                 xT=r2[c]["xT_out"], xch_in=xch, kv_in=r2[c]["kv_out"]) for c in cores]
    r3 = run_bass_kernel_spmd(nc3, maps, core_ids=cores).results
    out = np.empty((2, SEQ, D), np.float32)
    for c in cores:
        bi, qt = c // 4, c % 4
        y = r3[c]["xT_out"]
        out[bi, qt * TOK:(qt + 1) * TOK, :] = y.transpose(1, 0, 2).reshape(D, TOK).T
    return out
```

```python
import numpy as np
import ml_dtypes
import concourse.bass as bass
import concourse.mybir as mybir
from concourse.bass_utils import run_bass_kernel_spmd

F32 = mybir.dt.float32
BF16 = mybir.dt.bfloat16
ALU = mybir.AluOpType
AF = mybir.ActivationFunctionType

NL = 2
D = 1024
SEQ = 8192
TOK = 2048
T = 512
NT = TOK // T
NB = TOK // 128
EPS = 1e-6
CK = 31
PL = 156
NPAR = 2 * PL + 8
NSLOT = 3
DBG = 99
CM_MT = 0
CM_DT = 512
CM_W2 = 1024
CM_WT = 1028
CM_ID = 1092
CM_COEF = 1220
CM_SEL = 1252
CM_N = 1260
XW = 640


class Tl:
    all = []

    def __init__(self, psum=False):
        Tl.all.append(self)
        self.psum = psum
        self.w = None
        self.r = []
        self.sem = None
        self.cnt = 0


class B:
    def __init__(self, nc):
        self.nc = nc
        self.eng = {}
        for name, h in (("pe", nc.tensor), ("dve", nc.vector), ("act", nc.scalar),
                        ("pool", nc.gpsimd), ("sp", nc.sync)):
            sem = nc.alloc_semaphore("s_" + name)
            self.eng[name] = dict(h=h, sem=sem, cnt=0, seen={})
        self.nsem = 0

    def _waits(self, e, deps):
        E = self.eng[e]
        best = {}
        for d in deps:
            if d is None:
                continue
            sem, val = d
            if e == "pe" and sem is self.eng["pe"]["sem"]:
                continue
            k = id(sem)
            if k not in best or best[k][1] < val:
                best[k] = (sem, val)
        for k, (sem, val) in best.items():
            if E["seen"].get(k, 0) >= val:
                continue
            E["h"].wait_ge(sem, val)
            E["seen"][k] = val

    def _deps(self, R, W, deps, e=None):
        d = list(deps)
        for t in R:
            d.append(t.w)
            if t.psum:
                own = self.eng[e]["sem"]
                d.extend(tok for tok in t.r if tok[0] is not own)
        for t in W:
            d.append(t.w)
            d.extend(t.r)
        return d

    def _commit(self, tok, R, W):
        for t in R:
            t.r.append(tok)
        for t in W:
            t.w = tok
            t.r = []

    def op(self, e, fn, R=(), W=(), deps=()):
        E = self.eng[e]
        self._waits(e, self._deps(R, W, deps, e))
        ins = fn(E["h"])
        ins.then_inc(E["sem"], 1)
        E["cnt"] += 1
        tok = (E["sem"], E["cnt"])
        self._commit(tok, R, W)
        return tok

    def mm(self, Wt, out, pairs, R=(), start=True, stop=True, skip=False, deps=()):
        E = self.eng["pe"]
        self._waits("pe", self._deps(R, [Wt], deps, "pe"))
        n = len(pairs)
        ins = None
        for i, (l, r) in enumerate(pairs):
            kw = {}
            if skip:
                kw["skip_group_check"] = True
            ins = self.nc.tensor.matmul(out, l, r, start=(start and i == 0),
                                        stop=(stop and i == n - 1), **kw)
        ins.then_inc(E["sem"], 1)
        E["cnt"] += 1
        tok = (E["sem"], E["cnt"])
        self._commit(tok, R, [Wt])
        return tok

    def tr(self, Wt, out, in_, ident, R=()):
        E = self.eng["pe"]
        self._waits("pe", self._deps(R, [Wt], (), "pe"))
        ins = self.nc.tensor.transpose(out, in_, ident)
        ins.then_inc(E["sem"], 1)
        E["cnt"] += 1
        tok = (E["sem"], E["cnt"])
        self._commit(tok, R, [Wt])
        return tok

    def dma_multi(self, e, lst, t):
        E = self.eng[e]
        self._waits(e, self._deps((), [t], (), e))
        if t.sem is None:
            self.nsem += 1
            t.sem = self.nc.alloc_semaphore(f"d{self.nsem}")
        for out, in_ in lst:
            E["h"].dma_start(out=out, in_=in_).then_inc(t.sem, 16)
            t.cnt += 16
        tok = (t.sem, t.cnt)
        self._commit(tok, (), [t])
        return tok

    def dma(self, e, out, in_, R=(), W=(), deps=()):
        E = self.eng[e]
        self._waits(e, self._deps(R, W, deps, e))
        t = (list(W) + list(R))[0]
        if t.sem is None:
            self.nsem += 1
            t.sem = self.nc.alloc_semaphore(f"d{self.nsem}")
        E["h"].dma_start(out=out, in_=in_).then_inc(t.sem, 16)
        t.cnt += 16
        tok = (t.sem, t.cnt)
        self._commit(tok, R, W)
        return tok


def _bc(ap, shape, axes):
    for a in axes:
        ap = ap.unsqueeze(a)
    return ap.broadcast_to(list(shape))


def build(phases, store_x=False, final=False):
    nc = bass.Bass("TRN2", target_bir_lowering=False)
    Tl.all = []
    b = B(nc)
    lay = sorted({l for _, l in phases})
    has_p1 = [l for p, l in phases if p == "P1"]
    has_p2 = [l for p, l in phases if p == "P2"]

    def din(name, shape, dt=F32):
        return nc.dram_tensor(name, list(shape), dt, kind="ExternalInput").ap()

    def dout(name, shape, dt=F32):
        return nc.dram_tensor(name, list(shape), dt, kind="ExternalOutput").ap()

    w_in = {l: din(f"w_in{l}", [128, 8, 3072]) for l in lay}
    w_out = {l: din(f"w_out{l}", [128, 8, 1024]) for l in has_p2}
    w_ff1 = {l: din(f"w_ff1{l}", [128, 8, 4096]) for l in has_p2}
    w_ff2 = {l: din(f"w_ff2{l}", [128, 32, 1024]) for l in has_p2}
    params_d = din("params", [128, NPAR])
    ctab_d = din("ctab", [128, NB, 64])
    stab_d = din("stab", [128, NB, 64])
    cmisc_d = din("cmisc", [128, CM_N])
    xT_d = din("xT", [128, 8, TOK])
    if has_p2:
        xch_in_d = din("xch_in", [128, 8, XW])
        kv_in_d = din("kv_in", [128, NB, 2, 512], BF16)
    if has_p1:
        xch_out_d = dout("xch_out", [128, XW])
        kv_out_d = dout("kv_out", [128, NB, 2, 512], BF16)
    if store_x or final:
        xT_out_d = dout("xT_out", [128, 8, TOK])

    def sb(name, shape, dt=F32):
        return nc.alloc_sbuf_tensor(name, list(shape), dt)

    def ps(name, shape, dt=F32):
        return nc.alloc_psum_tensor(name, list(shape), dt)

    params = sb("params_sb", [128, NPAR]); params_t = Tl()
    ctab = sb("ctab_sb", [128, NB, 64]); ctab_t = Tl()
    stab = sb("stab_sb", [128, NB, 64]); stab_t = Tl()
    cmisc = sb("cmisc_sb", [128, CM_N]); cmisc_t = Tl()
    ident = sb("ident_sb", [128, 128], BF16); ident_t = Tl()
    ones_d = sb("ones_d", [128, 128], BF16)
    ones_c = sb("ones_c", [128, 128], BF16)
    ones_h = sb("ones_h", [128, 128], BF16)
    c_toks = [
        b.dma("sp", params[:, :], params_d, W=[params_t]),
        b.dma("sp", ctab[:, :, :], ctab_d, W=[ctab_t]),
        b.dma("sp", stab[:, :, :], stab_d, W=[stab_t]),
        b.dma("sp", cmisc[:, :], cmisc_d, W=[cmisc_t]),
        b.dma("pool", ident[:, :], cmisc_d[:, CM_ID:CM_ID + 128], W=[ident_t]),
    ]
    c_toks.append(b.op("dve", lambda h: h.memset(ones_d[:, :], 1.0 / 1024)))
    c_toks.append(b.op("dve", lambda h: h.memset(ones_c[:, :], 1.0 / 512)))
    c_toks.append(b.op("dve", lambda h: h.memset(ones_h[:, :], 1.0 / 128)))
    for e in ("pe", "dve", "act"):
        b._waits(e, c_toks)
    MT = cmisc[:, CM_MT:CM_MT + 512]
    DTAB = cmisc[:, CM_DT:CM_DT + 512]
    W2TAB = cmisc[:, CM_W2:CM_W2 + 4]

    def pcol(l, off):
        return params[:, l * PL + off: l * PL + off + 1]

    Xb = [sb(f"X{i}", [128, 8, T]) for i in range(2)]
    Xt_ = [Tl() for _ in range(2)]
    xn = sb("xn", [128, 8, T], BF16); xn_t = Tl()
    xnF = sb("xnF", [128, 8, T], BF16); xnF_t = Tl()
    xnD = sb("xnD", [128, 8, T], BF16) if (has_p1 and has_p2) else xn
    xnD_t = Tl() if (has_p1 and has_p2) else xn_t
    sq = [sb(f"sq{i}", [128, T], BF16) for i in range(4)]; sq_t = [Tl() for _ in range(4)]
    rstd = sb("rstd", [128, T]); rstd_t = Tl()
    rt1 = sb("rt1", [128, 512]); rt1_t = Tl()
    rt2 = sb("rt2", [128, 512]); rt2_t = Tl()
    slots = [sb(f"wslot{i}", [128, 4096], BF16) for i in range(NSLOT)]
    slot_t = [Tl() for _ in range(NSLOT)]
    npool = 8 - 3 - (1 if has_p1 else 0)
    pa = [ps(f"pa{i}", [128, 512]) for i in range(npool)]; pa_t = [Tl(True) for _ in range(npool)]
    if has_p1:
        pS1 = ps("pS1", [128, 512]); pS1_t = Tl(True)
    pS = ps("pS", [128, 512]); pS_t = Tl(True)
    po = ps("po", [128, 512]); po_t = Tl(True)
    ptr_all = ps("ptr", [128, 1024], BF16)
    ptr = [ptr_all[:, 0:512], ptr_all[:, 512:1024]]; ptr_t = [Tl(True)] * 2
    pa_rr = [0]

    def next_pa():
        i = pa_rr[0] % npool
        pa_rr[0] += 1
        return pa[i], pa_t[i]

    if has_p1:
        ktok = sb("ktok", [128, 4, 512], BF16); ktok_t = [Tl() for _ in range(4)]
        vtok = [sb("vtok0", [128, 512], BF16)] * 2; vtok_t = [Tl()] * 2
        vW = [sb(f"vW{i}", [128, 512], BF16) for i in range(2)]; vW_t = [Tl(), Tl()]
        xch_sb = sb("xch_sb", [128, XW]); xch_sb_t = Tl()
        sgt = sb("sgt", [128, 128]); sgt_t = Tl()
    if has_p2:
        xr = [sb(f"xr{i}", [128, XW]) for i in range(2)]; xr_t = [Tl(), Tl()]
        hg = sb("hg", [128, 4, 30 + T], BF16); hg_t = [Tl() for _ in range(4)]; halo_t = Tl()
        dg = sb("dg", [128, CK, 128], BF16); dg_t = Tl()
        sig = [sb(f"sig{i}", [128, T]) for i in range(2)]; sig_t = [Tl(), Tl()]
        acc = sb("acc", [128, 4, T]); acc_t = [Tl() for _ in range(4)]
        abf = [sb(f"abf{i}", [128, T], BF16) for i in range(2)]; abf_t = [Tl(), Tl()]
        asq = [sb(f"asq{i}", [128, T], BF16) for i in range(2)]; asq_t = [Tl(), Tl()]
        mean = sb("mean", [128, T]); mean_t = Tl()
        rstdc = sb("rstdc", [128, T]); rstdc_t = Tl()
        lnt = [sb(f"lnt{i}", [128, T]) for i in range(2)]; lnt_t = [Tl(), Tl()]
        Abuf = sb("Abuf", [128, 4, T], BF16); A_t = [Tl() for _ in range(4)]
        Bbuf = sb("Bbuf", [128, 4, T], BF16); Bb_t = Tl()
        qtok = sb("qtok", [128, 4, 512], BF16); qtok_t = [Tl() for _ in range(4)]
        sgn = sb("sgn", [128, 4, T]); sgn_t = [Tl() for _ in range(4)]
        kvt = [sb("kvt0", [128, 4, 2, 512], BF16)] * 2; kvt_t = [Tl()] * 2
        qT = sb("qT", [128, 512], BF16); qT_t = Tl()
        qdT = sb("qdT", [128, 512], BF16); qdT_t = Tl()
        kT = sb("kT", [128, 512], BF16); kT_t = Tl()
        PTm = sb("PTm", [128, 512], BF16); PTm_t = Tl()
        vw2 = sb("vw2", [128, 512], BF16); vw2_t = Tl()
        S = sb("S", [128, 512]); S_t = Tl()
        Sb_ = sb("Sb", [128, 512], BF16); Sb_t = Tl()
        osq = sb("osq", [128, 512], BF16); osq_t = Tl()
        rstdo = sb("rstdo", [128, 512]); rstdo_t = Tl()
        bt = sb("bt", [128, 512]); bt_t = Tl()
        H = sb("H", [128, 8, T], BF16); H_t = [Tl() for _ in range(8)]
        relu_s = lnt; relu_t = lnt_t
    if final:
        yout = sb("yout", [128, 4, T]); yout_t = Tl()

    steps = []
    cur = [steps]

    def step(piece, fn, gap=False):
        cur[0].append((piece, fn, gap))

    def run_steps():
        pidx = [i for i, s in enumerate(steps) if s[0] is not None]
        resident = [None] * NSLOT
        assign = []
        nxt = 0
        for j, i in enumerate(pidx):
            key = tuple((c0, c1, str(src)) for (c0, c1, src) in steps[i][0])
            if key in resident:
                assign.append((resident.index(key), False))
            else:
                si = nxt % NSLOT
                nxt += 1
                resident[si] = key
                assign.append((si, True))
        issued = 0

        def issue(j):
            si, load = assign[j]
            if not load:
                return
            lst = []
            for (c0, c1, src) in steps[pidx[j]][0]:
                dst = slots[si][:, c0:c1]
                if len(src.shape) == 3:
                    dst = dst.rearrange("p (a n) -> p a n", a=src.shape[1])
                lst.append((dst, src))
            b.dma_multi("pool", lst, slot_t[si])

        k = 0
        for i, (piece, fn, gap) in enumerate(steps):
            if piece is not None:
                while issued < len(pidx) and issued <= k + NSLOT - 1:
                    if assign[issued][1] and any(assign[h][0] == assign[issued][0] for h in range(k, issued)):
                        break
                    issue(issued)
                    issued += 1
                assert issued > k
                si = assign[k][0]
                fn(slots[si], slot_t[si])
                k += 1
            else:
                fn(None, None)

    rq = sb("rq", [128, 512]); rq_t = Tl()

    def rsqrt(dst, dst_t, src, src_t):
        b.op("act", lambda h: h.activation(rq[:, :], src, AF.Ln, bias=EPS), R=[src_t], W=[rq_t])
        b.op("act", lambda h: h.activation(dst, rq[:, :], AF.Exp, scale=-0.5), R=[rq_t], W=[dst_t])

    def square(kc, j, Xap, X_tl):
        if kc % 2 == 0:
            b.op("act", lambda h: h.activation(sq[j][:, :], Xap[:, kc, :], AF.Square), R=[X_tl], W=[sq_t[j]])
        else:
            b.op("dve", lambda h: h.tensor_tensor(sq[j][:, :], Xap[:, kc, :], Xap[:, kc, :], ALU.mult),
                 R=[X_tl], W=[sq_t[j]])

    def norm_stats_a(Xap, X_tl):
        for kc in range(4):
            square(kc, kc, Xap, X_tl)

    def norm_stats_b(Xap, X_tl, bank):
        p, p_t = bank
        for kc in range(4):
            b.mm(p_t, p[:, :], [(ones_d[:, :], sq[kc][:, :])], R=[sq_t[kc]], start=(kc == 0), stop=False)
        for kc in range(4, 8):
            square(kc, kc - 4, Xap, X_tl)

    def norm_stats_c(bank):
        p, p_t = bank
        for kc in range(4, 8):
            b.mm(p_t, p[:, :], [(ones_d[:, :], sq[kc - 4][:, :])], R=[sq_t[kc - 4]], start=False, stop=(kc == 7))
        rsqrt(rstd[:, :], rstd_t, p[:, :], p_t)

    def norm_apply(Xap, X_tl, goff, out3, out_tl, rng=range(8), ko=0):
        for kc in rng:
            b.op("dve", lambda h: h.scalar_tensor_tensor(
                out3[:, kc - ko, :], Xap[:, kc, :], params[:, goff + kc: goff + kc + 1], rstd[:, :],
                ALU.mult, ALU.mult), R=[X_tl, rstd_t], W=[out_tl])

    def norm_steps(Xap, X_tl, goff, out3, out_tl):
        def f(s, st):
            bank = next_pa()
            norm_stats_a(Xap, X_tl)
            norm_stats_b(Xap, X_tl, bank)
            norm_stats_c(bank)
            norm_apply(Xap, X_tl, goff, out3, out_tl)
        step(None, f)

    def rotary(psrc, psrc_t, blk_g, dst, dst_t):
        x4 = psrc.rearrange("p (h two d) -> p h two d", h=4, two=2)
        c = ctab[:, blk_g, :]
        s = stab[:, blk_g, :]
        t1v = rt1[:, :].rearrange("p (h two d) -> p h two d", h=4, two=2)
        t2v = rt2[:, :].rearrange("p (h two d) -> p h two d", h=4, two=2)
        b.op("dve", lambda h: h.tensor_tensor(t1v, x4, _bc(c, [128, 4, 2, 64], (1, 1)), ALU.mult),
             R=[psrc_t], W=[rt1_t])
        b.op("dve", lambda h: h.scalar_tensor_tensor(
            t2v[:, :, 0, :], x4[:, :, 1, :], -1.0, _bc(s, [128, 4, 64], (1,)), ALU.mult, ALU.mult),
            R=[psrc_t], W=[rt2_t])
        b.op("dve", lambda h: h.tensor_tensor(
            t2v[:, :, 1, :], x4[:, :, 0, :], _bc(s, [128, 4, 64], (1,)), ALU.mult),
            R=[psrc_t], W=[rt2_t])
        b.op("dve", lambda h: h.tensor_tensor(dst, rt1[:, :], rt2[:, :], ALU.add),
             R=[rt1_t, rt2_t], W=[dst_t])

    def win_piece(l, c0, n=512):
        return [(0, 8 * n, w_in[l][:, :, c0:c0 + n])]

    def glu_piece(l, hp):
        return [(0, 2048, w_in[l][:, :, hp * 256:hp * 256 + 256]),
                (2048, 4096, w_in[l][:, :, 512 + hp * 256:512 + hp * 256 + 256])]

    def emit_P1_D0(l, t, Xap, X_tl):
        norm_steps(Xap, X_tl, l * PL + 0, xnD, xnD_t)

    def emit_P1(l, t):
        xn, xn_t = xnD, xnD_t

        def do_k(s, st):
            Wv = s[:, :].rearrange("p (a n) -> p a n", a=8)
            for blk in range(4):
                p, p_t = next_pa()
                b.mm(p_t, p[:, :], [(xn[:, kc, blk * 128:(blk + 1) * 128], Wv[:, kc, :]) for kc in range(8)],
                     R=[xn_t, st])
                rotary(p[:, :], p_t, t * 4 + blk, ktok[:, blk, :], ktok_t[blk])
                b.dma("sp", kv_out_d[:, t * 4 + blk, 0, :], ktok[:, blk, :], R=[ktok_t[blk]])
        step(win_piece(l, 1536), do_k)

        def do_v(s, st):
            Wv = s[:, :].rearrange("p (a n) -> p a n", a=8)

            def sfin(blk):
                gb = t * 4 + blk
                i = blk % 2
                for hh in range(4):
                    b.mm(pS1_t, pS1[:, hh * 128:(hh + 1) * 128],
                         [(ktok[:, blk, hh * 128:(hh + 1) * 128], vW[i][:, hh * 128:(hh + 1) * 128])],
                         R=[ktok_t[blk], vW_t[i]], start=(gb == 0 and hh == 0), stop=(gb == NB - 1 and hh == 3),
                         skip=True)
            for blk in range(4):
                gb = t * 4 + blk
                i = blk % 2
                p, p_t = next_pa()
                b.mm(p_t, p[:, :], [(xn[:, kc, blk * 128:(blk + 1) * 128], Wv[:, kc, :]) for kc in range(8)],
                     R=[xn_t, st])
                b.op("act", lambda h: h.copy(vtok[i][:, :], p[:, :]), R=[p_t], W=[vtok_t[i]])
                b.dma("sp", kv_out_d[:, gb, 1, :], vtok[i][:, :], R=[vtok_t[i]])
                wt = cmisc[:, CM_WT + gb * 4: CM_WT + gb * 4 + 4]
                b.op("dve", lambda h: h.tensor_tensor(
                    vW[i][:, :].rearrange("p (h e) -> p h e", h=4), p[:, :].rearrange("p (h e) -> p h e", h=4),
                    _bc(wt, [128, 4, 128], (2,)), ALU.mult), R=[p_t], W=[vW_t[i]])
                if blk >= 1:
                    sfin(blk - 1)
            sfin(3)
        step(win_piece(l, 2048), do_v)

        if t == NT - 1:
            banks = []
            for hp in range(2):
                def do_tail_piece(s, st, hp=hp):
                    if hp == 0:
                        banks.append(next_pa())
                        banks.append(next_pa())
                    (pA, pA_t), (pG, pG_t) = banks
                    Wa = s[:, 0:2048].rearrange("p (a n) -> p a n", a=8)
                    Wg = s[:, 2048:4096].rearrange("p (a n) -> p a n", a=8)
                    for j in range(2):
                        cc = hp * 2 + j
                        b.mm(pA_t, pA[:, cc * 32:(cc + 1) * 32],
                             [(Wa[:, kc, j * 128:(j + 1) * 128], xn[:, kc, T - 32:T]) for kc in range(8)],
                             R=[xn_t, st])
                        b.mm(pG_t, pG[:, cc * 32:(cc + 1) * 32],
                             [(Wg[:, kc, j * 128:(j + 1) * 128], xn[:, kc, T - 32:T]) for kc in range(8)],
                             R=[xn_t, st])
                    if hp == 1:
                        b.op("act", lambda h: h.activation(sgt[:, :], pG[:, 0:128], AF.Sigmoid),
                             R=[pG_t], W=[sgt_t])
                        b.op("dve", lambda h: h.tensor_tensor(xch_sb[:, 512:640], pA[:, 0:128], sgt[:, :], ALU.mult),
                             R=[pA_t, sgt_t], W=[xch_sb_t])
                step(glu_piece(l, hp), do_tail_piece)

            def fin(s, st):
                b.op("act", lambda h: h.copy(xch_sb[:, 0:512], pS1[:, :]), R=[pS1_t], W=[xch_sb_t])
                b.dma("sp", xch_out_d, xch_sb[:, :], R=[xch_sb_t])
            step(None, fin)

    def emit_P2_prologue(l):
        def pro(s, st):
            for r in range(8):
                x_ = xr[r % 2]
                x_t = xr_t[r % 2]
                b.dma("sp", x_[:, :], xch_in_d[:, r, :], W=[x_t])
                for hh in range(4):
                    cf = cmisc[:, CM_COEF + r * 4 + hh: CM_COEF + r * 4 + hh + 1]
                    src = x_[:, hh * 128:(hh + 1) * 128]
                    dst = S[:, hh * 128:(hh + 1) * 128]
                    if r == 0:
                        b.op("dve", lambda h: h.tensor_scalar(dst, src, cf, None, ALU.mult),
                             R=[x_t], W=[S_t])
                    else:
                        b.op("dve", lambda h: h.scalar_tensor_tensor(dst, src, cf, dst, ALU.mult, ALU.add),
                             R=[x_t], W=[S_t])
                sl = cmisc[:, CM_SEL + r: CM_SEL + r + 1]
                src = x_[:, 512:640].rearrange("p (c n) -> p c n", c=4)[:, :, 2:32]
                dst = hg[:, :, 0:30]
                if r == 0:
                    b.op("dve", lambda h: h.tensor_scalar(dst, src, sl, None, ALU.mult),
                         R=[x_t], W=[halo_t])
                else:
                    b.op("dve", lambda h: h.scalar_tensor_tensor(dst, src, sl, dst, ALU.mult, ALU.add),
                         R=[x_t], W=[halo_t])
            b.op("act", lambda h: h.copy(Sb_[:, :], S[:, :]), R=[S_t], W=[Sb_t])
        step(None, pro)

    def emit_P2_AB(l, t, Xap, X_tl):
        g1 = l * PL + 0
        cw0 = l * PL + 16
        cb0 = l * PL + 140
        lg0 = l * PL + 144
        lb0 = l * PL + 148
        ng0 = l * PL + 152
        kvb = kvt[t % 2]
        kvb_t = kvt_t[t % 2]

        step(None, lambda s, st: b.dma("sp", kvb[:, :, :, :], kv_in_d[:, t * 4:(t + 1) * 4, :, :], W=[kvb_t]))
        norm_steps(Xap, X_tl, g1, xn, xn_t)

        for hp in range(2):
            def do_glu(s, st, hp=hp):
                Wa = s[:, 0:2048].rearrange("p (a n) -> p a n", a=8)
                Wg = s[:, 2048:4096].rearrange("p (a n) -> p a n", a=8)
                for j in range(2):
                    cc = hp * 2 + j
                    p1, p1_t = next_pa()
                    p2, p2_t = next_pa()
                    b.mm(p1_t, p1[:, :], [(Wa[:, kc, j * 128:(j + 1) * 128], xn[:, kc, :]) for kc in range(8)],
                         R=[xn_t, st])
                    b.mm(p2_t, p2[:, :], [(Wg[:, kc, j * 128:(j + 1) * 128], xn[:, kc, :]) for kc in range(8)],
                         R=[xn_t, st])
                    i = cc % 2
                    b.op("act", lambda h: h.activation(sig[i][:, :], p2[:, :], AF.Sigmoid), R=[p2_t], W=[sig_t[i]])
                    b.op("dve", lambda h: h.tensor_tensor(hg[:, cc, 30:30 + T], p1[:, :], sig[i][:, :], ALU.mult),
                         R=[p1_t, sig_t[i]], W=[hg_t[cc]])
            step(glu_piece(l, hp), do_glu, gap=(hp == 0))

        def do_q(s, st):
            Wv = s[:, :].rearrange("p (a n) -> p a n", a=8)
            for blk in range(4):
                p, p_t = next_pa()
                b.mm(p_t, p[:, :], [(xn[:, kc, blk * 128:(blk + 1) * 128], Wv[:, kc, :]) for kc in range(8)],
                     R=[xn_t, st])
                rotary(p[:, :], p_t, t * 4 + blk, qtok[:, blk, :], qtok_t[blk])
        step(win_piece(l, 1024), do_q)

        def do_g(s, st):
            Wv = s[:, :].rearrange("p (a n) -> p a n", a=8)
            for hh in range(4):
                p, p_t = next_pa()
                b.mm(p_t, p[:, :], [(Wv[:, kc, hh * 128:(hh + 1) * 128], xn[:, kc, :]) for kc in range(8)],
                     R=[xn_t, st])
                i = hh % 2
                b.op("act", lambda h: h.activation(sig[i][:, :], p[:, :], AF.Sigmoid), R=[p_t], W=[sig_t[i]])
                b.op("dve", lambda h: h.scalar_tensor_tensor(
                    sgn[:, hh, :], p[:, :], params[:, ng0 + hh: ng0 + hh + 1], sig[i][:, :], ALU.mult, ALU.mult),
                    R=[p_t, sig_t[i]], W=[sgn_t[hh]])
        step(win_piece(l, 2560), do_g)

        for cc in range(4):
            def conv_build(s, st, cc=cc):
                wv = params[:, cw0 + cc * CK: cw0 + (cc + 1) * CK]
                b.op("dve", lambda h: h.tensor_tensor(
                    dg[:, :, :], _bc(ident[:, :], [128, CK, 128], (1,)), _bc(wv, [128, CK, 128], (2,)), ALU.mult),
                    W=[dg_t])
            step(None, conv_build)

            def conv_mm(s, st, cc=cc):
                p, p_t = next_pa()
                b.mm(p_t, p[:, :], [(dg[:, j, :], hg[:, cc, j:j + T]) for j in range(CK)],
                     R=[dg_t, hg_t[cc], halo_t])
                b.op("act", lambda h: h.activation(acc[:, cc, :], p[:, :], AF.Identity,
                                                   bias=params[:, cb0 + cc: cb0 + cc + 1]),
                     R=[p_t], W=[acc_t[cc]])
            step(None, conv_mm, gap=True)

        lnb = []

        def ln_a(s, st):
            b.op("dve", lambda h: h.tensor_copy(hg[:, :, 0:30], hg[:, :, T:T + 30]), R=hg_t, W=[halo_t])
            lnb.append(next_pa())
            lnb.append(next_pa())
            (pm, pm_t), (pq, pq_t) = lnb
            for cc in range(4):
                i = cc % 2
                b.op("act", lambda h: h.copy(abf[i][:, :], acc[:, cc, :]), R=[acc_t[cc]], W=[abf_t[i]])
                b.op("act", lambda h: h.activation(asq[i][:, :], acc[:, cc, :], AF.Square), R=[acc_t[cc]], W=[asq_t[i]])
                b.mm(pm_t, pm[:, :], [(ones_c[:, :], abf[i][:, :])], R=[abf_t[i]], start=(cc == 0), stop=(cc == 3))
                b.mm(pq_t, pq[:, :], [(ones_c[:, :], asq[i][:, :])], R=[asq_t[i]], start=(cc == 0), stop=(cc == 3))
            b.op("act", lambda h: h.copy(mean[:, :], pm[:, :]), R=[pm_t], W=[mean_t])
            b.op("dve", lambda h: h.tensor_tensor(lnt[0][:, :], mean[:, :], mean[:, :], ALU.mult),
                 R=[mean_t], W=[lnt_t[0]])
            b.op("dve", lambda h: h.tensor_tensor(lnt[1][:, :], pq[:, :], lnt[0][:, :], ALU.subtract),
                 R=[pq_t, lnt_t[0]], W=[lnt_t[1]])
            rsqrt(rstdc[:, :], rstdc_t, lnt[1][:, :], lnt_t[1])
        step(None, ln_a)

        for cc in range(4):
            def ln_c(s, st, cc=cc):
                b.op("dve", lambda h: h.tensor_tensor(lnt[0][:, :], acc[:, cc, :], mean[:, :], ALU.subtract),
                     R=[acc_t[cc], mean_t], W=[lnt_t[0]])
                b.op("dve", lambda h: h.tensor_tensor(lnt[1][:, :], lnt[0][:, :], rstdc[:, :], ALU.mult),
                     R=[lnt_t[0], rstdc_t], W=[lnt_t[1]])
                b.op("dve", lambda h: h.tensor_scalar(
                    lnt[0][:, :], lnt[1][:, :], params[:, lg0 + cc: lg0 + cc + 1],
                    params[:, lb0 + cc: lb0 + cc + 1], ALU.mult, ALU.add), R=[lnt_t[1]], W=[lnt_t[0]])
                i = cc % 2
                b.op("act", lambda h: h.activation(sig[i][:, :], lnt[0][:, :], AF.Sigmoid),
                     R=[lnt_t[0]], W=[sig_t[i]])
                b.op("dve", lambda h: h.tensor_tensor(Abuf[:, cc, :], lnt[0][:, :], sig[i][:, :], ALU.mult),
                     R=[lnt_t[0], sig_t[i]], W=[A_t[cc]])
            step(None, ln_c)

        gam = [1.0 - 2.0 ** (-5 - hh) for hh in range(4)]
        for blk in range(4):
            kb = kvb[:, blk, 0, :]
            vb = kvb[:, blk, 1, :]
            st8 = {}

            def ret_T(s, st, blk=blk, kb=kb, vb=vb):
                for hh in range(4):
                    b.tr(ptr_t[0], ptr[0][:, hh * 128:(hh + 1) * 128], qtok[:, blk, hh * 128:(hh + 1) * 128],
                         ident[:, :], R=[qtok_t[blk]])
                for hh in range(4):
                    b.tr(ptr_t[1], ptr[1][:, hh * 128:(hh + 1) * 128], kb[:, hh * 128:(hh + 1) * 128],
                         ident[:, :], R=[kvb_t])
                b.op("act", lambda h: h.copy(kT[:, :], ptr[1][:, :]), R=[ptr_t[1]], W=[kT_t])
                b.op("act", lambda h: h.copy(qT[:, :], ptr[0][:, :]), R=[ptr_t[0]], W=[qT_t])
                b.op("dve", lambda h: h.tensor_tensor(qdT[:, :], ptr[0][:, :], DTAB, ALU.mult),
                     R=[ptr_t[0]], W=[qdT_t])
                b.op("dve", lambda h: h.tensor_tensor(
                    vw2[:, :].rearrange("p (h e) -> p h e", h=4), vb.rearrange("p (h e) -> p h e", h=4),
                    _bc(W2TAB, [128, 4, 128], (2,)), ALU.mult), R=[kvb_t], W=[vw2_t])
            step(None, ret_T)

            def ret_P(s, st, blk=blk, kb=kb, vb=vb, st8=st8):
                p, p_t = next_pa()
                for hh in range(4):
                    c = slice(hh * 128, (hh + 1) * 128)
                    b.mm(p_t, p[:, c], [(kT[:, c], qT[:, c])], R=[kT_t, qT_t])
                b.op("dve", lambda h: h.tensor_tensor(PTm[:, :], p[:, :], MT, ALU.mult), R=[p_t], W=[PTm_t])
                for hh in range(4):
                    c = slice(hh * 128, (hh + 1) * 128)
                    b.mm(pS_t, pS[:, c], [(kb[:, c], vw2[:, c])], R=[kvb_t, vw2_t])
            step(None, ret_P, gap=True)

            def ret_O(s, st, blk=blk, kb=kb, vb=vb, st8=st8):
                for hh in range(4):
                    c = slice(hh * 128, (hh + 1) * 128)
                    b.mm(po_t, po[:, c], [(vb[:, c], PTm[:, c]), (Sb_[:, c], qdT[:, c])],
                         R=[kvb_t, PTm_t, Sb_t, qdT_t])
                for hh in range(4):
                    c = slice(hh * 128, (hh + 1) * 128)
                    b.op("dve", lambda h: h.scalar_tensor_tensor(
                        S[:, c], S[:, c], float(gam[hh] ** 128), pS[:, c], ALU.mult, ALU.add),
                        R=[pS_t], W=[S_t])
                b.op("act", lambda h: h.copy(Sb_[:, :], S[:, :]), R=[S_t], W=[Sb_t])
                b.op("act", lambda h: h.activation(osq[:, :], po[:, :], AF.Square), R=[po_t], W=[osq_t])
            step(None, ret_O, gap=True)

            def ret_N(s, st, blk=blk, st8=st8):
                pn, pn_t = next_pa()
                b.mm(pn_t, pn[:, :], [(ones_h[:, :], osq[:, :])], R=[osq_t])
                rsqrt(rstdo[:, :], rstdo_t, pn[:, :], pn_t)
                b.op("dve", lambda h: h.tensor_tensor(bt[:, :], po[:, :], rstdo[:, :], ALU.mult),
                     R=[po_t, rstdo_t], W=[bt_t])
                b.op("dve", lambda h: h.tensor_tensor(
                    Bbuf[:, :, blk * 128:(blk + 1) * 128], bt[:, :].rearrange("p (h i) -> p h i", h=4),
                    sgn[:, :, blk * 128:(blk + 1) * 128], ALU.mult), R=[bt_t] + sgn_t, W=[Bb_t])
            step(None, ret_N, gap=True)

    def emit_P2_C(l, t, Xap, X_tl):
        g2 = l * PL + 8
        for half in range(2):
            def do_out(s, st, half=half):
                Wv = s[:, :].rearrange("p (a n) -> p a n", a=8)
                for j in range(4):
                    oc = half * 4 + j
                    p, p_t = next_pa()
                    pairs = []
                    for kc in range(8):
                        rhs = Abuf[:, kc, :] if kc < 4 else Bbuf[:, kc - 4, :]
                        pairs.append((Wv[:, kc, j * 128:(j + 1) * 128], rhs))
                    b.mm(p_t, p[:, :], pairs, R=A_t + [Bb_t, st])
                    b.op("dve", lambda h: h.tensor_tensor(Xap[:, oc, :], Xap[:, oc, :], p[:, :], ALU.add),
                         R=[p_t], W=[X_tl])
            step([(0, 4096, w_out[l][:, :, half * 512:(half + 1) * 512])], do_out)

        norm_steps(Xap, X_tl, g2, xnF, xnF_t)
        for grp in range(4):
            for pc in range(2):
                def do_f1(s, st, grp=grp, pc=pc):
                    Wv = s[:, :].rearrange("p (a n) -> p a n", a=8)
                    for j in range(4):
                        hc = pc * 4 + j
                        p, p_t = next_pa()
                        b.mm(p_t, p[:, :], [(Wv[:, kc, j * 128:(j + 1) * 128], xnF[:, kc, :]) for kc in range(8)],
                             R=[xnF_t, st])
                        i = hc % 2
                        b.op("act", lambda h: h.activation(relu_s[i][:, :], p[:, :], AF.Relu), R=[p_t], W=[relu_t[i]])
                        b.op("act", lambda h: h.activation(H[:, hc, :], relu_s[i][:, :], AF.Square),
                             R=[relu_t[i]], W=[H_t[hc]])
                c0 = (grp * 2 + pc) * 512
                step([(0, 4096, w_ff1[l][:, :, c0:c0 + 512])], do_f1)
            for pc in range(2):
                def do_f2(s, st, grp=grp, pc=pc):
                    Wv = s[:, :].rearrange("p (a n) -> p a n", a=8)
                    for j in range(4):
                        oc = pc * 4 + j
                        p, p_t = next_pa()
                        b.mm(p_t, p[:, :], [(Wv[:, k2, j * 128:(j + 1) * 128], H[:, k2, :]) for k2 in range(8)],
                             R=H_t + [st])
                        b.op("dve", lambda h: h.tensor_tensor(Xap[:, oc, :], Xap[:, oc, :], p[:, :], ALU.add),
                             R=[p_t], W=[X_tl])
                step([(0, 4096, w_ff2[l][:, grp * 8:(grp + 1) * 8, pc * 512:(pc + 1) * 512])], do_f2)

    for p, l in phases:
        if p == "P2":
            emit_P2_prologue(l)
    AB = [[] for _ in range(NT)]
    C = [[] for _ in range(NT)]
    Dl = [[] for _ in range(NT)]
    D1 = [[] for _ in range(NT)]
    for t in range(NT):
        Xap = Xb[t % 2]
        X_tl = Xt_[t % 2]

        def ldx(s, st, t=t, Xap=Xap, X_tl=X_tl):
            b.dma("sp", Xap[:, :, :], xT_d[:, :, t * T:(t + 1) * T], W=[X_tl])
        cur[0] = AB[t]
        step(None, ldx)
        for p, l in phases:
            if p == "P2":
                cur[0] = AB[t]
                emit_P2_AB(l, t, Xap, X_tl)
                cur[0] = C[t]
                emit_P2_C(l, t, Xap, X_tl)
        cur[0] = Dl[t]
        for p, l in phases:
            if p == "P1":
                emit_P1_D0(l, t, Xap, X_tl)
        if final:
            def fin_norm(s, st, t=t, Xap=Xap, X_tl=X_tl):
                bank = next_pa()
                norm_stats_a(Xap, X_tl)
                norm_stats_b(Xap, X_tl, bank)
                norm_stats_c(bank)
                for hf in range(2):
                    norm_apply(Xap, X_tl, 2 * PL, yout, yout_t, rng=range(hf * 4, hf * 4 + 4), ko=hf * 4)
                    b.dma("sp", xT_out_d[:, hf * 4:(hf + 1) * 4, t * T:(t + 1) * T], yout[:, :, :], R=[yout_t])
            step(None, fin_norm)
        elif store_x:
            def stx(s, st, t=t, Xap=Xap, X_tl=X_tl):
                b.dma("sp", xT_out_d[:, :, t * T:(t + 1) * T], Xap[:, :, :], R=[X_tl])
            step(None, stx)
        cur[0] = D1[t]
        for p, l in phases:
            if p == "P1":
                emit_P1(l, t)
    cur[0] = steps

    def merge(s1, s2):
        ngap = sum(1 for x in s2 if x[2])
        extra = max(0, len(s1) - ngap)
        i1 = 0
        for j, x in enumerate(s2):
            want = 1 if x[2] else 0
            want += ((j + 1) * extra) // max(1, len(s2)) - (j * extra) // max(1, len(s2))
            for _ in range(want):
                if i1 < len(s1):
                    steps.append(s1[i1])
                    i1 += 1
            steps.append(x)
        steps.extend(s1[i1:])

    if not has_p2:
        for t in range(NT):
            steps.extend(AB[t] + Dl[t] + D1[t])
    else:
        steps.extend(AB[0])
        for t in range(NT):
            s2 = (D1[t - 1] if t >= 1 else []) + (AB[t + 1] if t + 1 < NT else [])
            merge(C[t], s2)
            steps.extend(Dl[t])
        steps.extend(D1[NT - 1])
    run_steps()
    fin_deps = []
    for tl in Tl.all:
        fin_deps.append(tl.w)
        fin_deps.extend(tl.r)
    b._waits("sp", fin_deps)
    return nc


def _fm(a, kchunks):
    K, N = a.shape
    return np.ascontiguousarray(a.reshape(kchunks, 128, N).transpose(1, 0, 2))


def _consts(core):
    qt = core % 4
    gam = np.array([1.0 - 2.0 ** (-5 - h) for h in range(4)], np.float64)
    s = 128.0 ** -0.5
    inv_freq = (10000.0 ** (-np.arange(0, 128, 2, dtype=np.float32) / np.float32(128))).astype(np.float32)
    pos = (qt * TOK + np.arange(TOK, dtype=np.float32)).astype(np.float32)
    ang = (pos[:, None] * inv_freq[None, :]).astype(np.float32)
    c = np.cos(ang).astype(np.float32).reshape(NB, 128, 64).transpose(1, 0, 2)
    sn = np.sin(ang).astype(np.float32).reshape(NB, 128, 64).transpose(1, 0, 2)
    cm = np.zeros((128, CM_N), np.float32)
    i = np.arange(128)
    ii, jj = np.meshgrid(i, i, indexing="ij")
    for h in range(4):
        same = (ii // 64) == (jj // 64)
        causal = (ii // 64 == 1) & (jj // 64 == 0)
        M = np.where(same, gam[h] ** np.abs(ii - jj), np.where(causal, gam[h] ** np.maximum(ii - jj, 0), 0.0))
        cm[:, CM_MT + h * 128: CM_MT + (h + 1) * 128] = (s * M.T).astype(np.float32)
        cm[:, CM_DT + h * 128: CM_DT + (h + 1) * 128] = (gam[h] ** (i + 1.0))[None, :]
        cm[:, CM_W2 + h] = s * gam[h] ** (127.0 - i)
        for gb in range(NB):
            cm[:, CM_WT + gb * 4 + h] = s * gam[h] ** (2047.0 - (gb * 128 + i))
        for r in range(8):
            if r // 4 == core // 4 and r < core:
                cm[:, CM_COEF + r * 4 + h] = gam[h] ** (2048.0 * (core - 1 - r))
    cm[:, CM_ID:CM_ID + 128] = np.eye(128, dtype=np.float32)
    if qt > 0:
        cm[:, CM_SEL + core - 1] = 1.0
    return np.ascontiguousarray(c), np.ascontiguousarray(sn), cm


def _params(norm1_g, conv_w, conv_b, conv_ln_g, conv_ln_b, ret_norm_g, norm2_g, final_g):
    P = np.zeros((128, NPAR), np.float32)
    for l in range(NL):
        o = l * PL
        P[:, o:o + 8] = norm1_g[l].reshape(8, 128).T
        P[:, o + 8:o + 16] = norm2_g[l].reshape(8, 128).T
        cw = conv_w[l].reshape(CK, 4, 128)
        P[:, o + 16:o + 16 + 4 * CK] = cw.transpose(2, 1, 0).reshape(128, 4 * CK)
        P[:, o + 140:o + 144] = conv_b[l].reshape(4, 128).T
        P[:, o + 144:o + 148] = conv_ln_g[l].reshape(4, 128).T
        P[:, o + 148:o + 152] = conv_ln_b[l].reshape(4, 128).T
        P[:, o + 152:o + 156] = ret_norm_g[l].reshape(4, 128).T
    P[:, 2 * PL:2 * PL + 8] = final_g.reshape(8, 128).T
    return P


_NC_CACHE = {}


def _get_nc(key, *a, **k):
    if key not in _NC_CACHE:
        _NC_CACHE[key] = build(*a, **k)
    return _NC_CACHE[key]


def kernel(x, norm1_g, w_in, conv_w, conv_b, conv_ln_g, conv_ln_b, ret_norm_g,
           w_out, norm2_g, w_ff1, w_ff2, final_g):
    f = lambda a: np.asarray(a, dtype=np.float32)
    x, norm1_g, w_in, conv_w, conv_b = f(x), f(norm1_g), f(w_in), f(conv_w), f(conv_b)
    conv_ln_g, conv_ln_b, ret_norm_g, w_out = f(conv_ln_g), f(conv_ln_b), f(ret_norm_g), f(w_out)
    norm2_g, w_ff1, w_ff2, final_g = f(norm2_g), f(w_ff1), f(w_ff2), f(final_g)
    ncores = 8
    cores = list(range(ncores))
    P = _params(norm1_g, conv_w, conv_b, conv_ln_g, conv_ln_b, ret_norm_g, norm2_g, final_g)
    Win = [_fm(w_in[l], 8) for l in range(NL)]
    Wout = [_fm(w_out[l], 8) for l in range(NL)]
    Wf1 = [_fm(w_ff1[l], 8) for l in range(NL)]
    Wf2 = [_fm(w_ff2[l], 32) for l in range(NL)]
    cst = [_consts(c) for c in cores]
    xT = []
    for c in cores:
        bi, qt = c // 4, c % 4
        xs = x[bi, qt * TOK:(qt + 1) * TOK, :]
        xT.append(_fm(np.ascontiguousarray(xs.T), 8))

    def base(c):
        return {"params": P, "ctab": cst[c][0], "stab": cst[c][1], "cmisc": cst[c][2]}

    nc1 = _get_nc("L1", [("P1", 0)])
    maps = [dict(base(c), w_in0=Win[0], xT=xT[c]) for c in cores]
    r1 = run_bass_kernel_spmd(nc1, maps, core_ids=cores).results
    xch = np.ascontiguousarray(np.stack([r1[c]["xch_out"] for c in cores], 0).transpose(1, 0, 2))
    nc2 = _get_nc("L2", [("P2", 0), ("P1", 1)], store_x=True)
    maps = [dict(base(c), w_in0=Win[0], w_in1=Win[1], w_out0=Wout[0], w_ff10=Wf1[0], w_ff20=Wf2[0],
                 xT=xT[c], xch_in=xch, kv_in=r1[c]["kv_out"]) for c in cores]
    r2 = run_bass_kernel_spmd(nc2, maps, core_ids=cores).results
    xch = np.ascontiguousarray(np.stack([r2[c]["xch_out"] for c in cores], 0).transpose(1, 0, 2))
    nc3 = _get_nc("L3", [("P2", 1)], final=True)
    maps = [dict(base(c), w_in1=Win[1], w_out1=Wout[1], w_ff11=Wf1[1], w_ff21=Wf2[1],
                 xT=r2[c]["xT_out"], xch_in=xch, kv_in=r2[c]["kv_out"]) for c in cores]
    r3 = run_bass_kernel_spmd(nc3, maps, core_ids=cores).results
    out = np.empty((2, SEQ, D), np.float32)
    for c in cores:
        bi, qt = c // 4, c % 4
        y = r3[c]["xT_out"]
        out[bi, qt * TOK:(qt + 1) * TOK, :] = y.transpose(1, 0, 2).reshape(D, TOK).T
    return out
```
